# Optimizing a Trainium2 kernel written in Bass

```python
import jax
import jax.numpy as jnp
from jax import lax
import numpy as np

D_MODEL = 2048
BATCH = 32
SEQ = 256
DEPTH = 4
DEC_BATCH = 8
DEC_SEQ = 2048
PAST_LEN = 256

GRID_W = 64
RMS_EPS = 1e-6
N_EVEN = (DEPTH + 1) // 2
N_ODD = DEPTH // 2
A_WIDTH = D_MODEL // 2
A_HEAD = 64
A_HEADS = A_WIDTH // A_HEAD
LORA = 64
GN_EPS = 64e-5
B_WIDTH = D_MODEL - A_WIDTH
B_GROUPS = 4
B_GROUP_CH = B_WIDTH // B_GROUPS
SHIFT_COLS = 3 * A_WIDTH + 4 * LORA
EVEN_IN = SHIFT_COLS + A_WIDTH + 2 * B_WIDTH
C_HEAD = 64
C_HEADS = D_MODEL // C_HEAD
C_KV_HEADS = C_HEADS // 4
C_GROUP = C_HEADS // C_KV_HEADS
WINDOW = 128
BLOCK = 128
ROPE_BASE = 10000.0
ODD_IN = (C_HEADS + 2 * C_KV_HEADS) * C_HEAD + D_MODEL
NEG_INF = -1e30

kernel_name = 'hybrid_rwkv7_fnet_swa_dit_step'


def _rmsnorm(x, g):
    x32 = x.astype(jnp.float32)
    y = x32 * lax.rsqrt(jnp.mean(x32 * x32, axis=-1, keepdims=True) + RMS_EPS)
    return (y * g.astype(jnp.float32)).astype(x.dtype)


def _modulation(cond, w, b):
    m = (jax.nn.silu(cond) @ w + b)[..., None, :]
    return jnp.split(m, 3, axis=-1)


def _centred_shift(p):
    zero = jnp.zeros_like(p[:, :1])
    prev = jnp.concatenate([zero, p[:, :-1]], axis=1)
    nxt = jnp.concatenate([p[:, 1:], zero], axis=1)
    return 0.5 * (prev + nxt)


def _heads(t):
    return t.reshape(t.shape[:-1] + (A_HEADS, A_HEAD))


def _rev(t):
    return t[:, ::-1]


def _dirs(t):
    return jnp.stack([t[0], _rev(t[1])])


def _both(t):
    return jnp.stack([t, _rev(t)])


def _wkv_scan(w, kk, kka, k, v, r, s0):
    def step(s, inp):
        w_t, kk_t, kka_t, k_t, v_t, r_t = inp
        s = (s * w_t[..., None, :]
             - jnp.einsum('dbhij,dbhj->dbhi', s, kk_t)[..., None] * kka_t[..., None, :]
             + v_t[..., :, None] * k_t[..., None, :])
        return s, jnp.einsum('dbhij,dbhj->dbhi', s, r_t)
    xs = tuple(jnp.moveaxis(a, 2, 0) for a in (w, kk, kka, k, v, r))
    s_fin, o = lax.scan(step, s0, xs)
    return jnp.moveaxis(o, 0, 2), s_fin


def _rwkv_fourier_mixer(h, s0, w_in, mu, w0, w_up, a0, a_up, k_k, k_a, r_k, gn_w, gn_b, w_out):
    bsz, t_len, _ = h.shape
    f32 = jnp.float32
    proj = h @ w_in
    sh = proj[..., :SHIFT_COLS]
    sh = sh + mu * (_centred_shift(sh) - sh)
    r = sh[..., :A_WIDTH].astype(f32)
    k = sh[..., A_WIDTH:2 * A_WIDTH].astype(f32)
    v = sh[..., 2 * A_WIDTH:3 * A_WIDTH].astype(f32)
    low = sh[..., 3 * A_WIDTH:].reshape(bsz, t_len, 2, 2, LORA)
    w_low, a_low = low[:, :, 0], low[:, :, 1]
    o0 = SHIFT_COLS
    gate_a = proj[..., o0:o0 + A_WIDTH].astype(f32)
    u = proj[..., o0 + A_WIDTH:o0 + A_WIDTH + B_WIDTH].astype(f32)
    gate_b = proj[..., o0 + A_WIDTH + B_WIDTH:].astype(f32)

    w_raw = (w0[:, None, None] + jnp.einsum('btdl,dla->dbta', jnp.tanh(w_low), w_up)).astype(f32)
    decay = jnp.exp(-jnp.exp(-jax.nn.softplus(-w_raw) - 0.5))
    a = jax.nn.sigmoid((a0[:, None, None] + jnp.einsum('btdl,dla->dbta', a_low, a_up)).astype(f32))
    k_dir = k[None] * (1.0 + (a - 1.0) * k_a.astype(f32))
    kk = _heads(k * k_k.astype(f32))
    kk = kk / jnp.maximum(jnp.linalg.norm(kk, axis=-1, keepdims=True), 1e-12)
    rh, vh = _heads(r), _heads(v)
    o, s_fin = _wkv_scan(_dirs(_heads(decay)), _both(kk), _dirs(kk[None] * _heads(a)),
                         _dirs(_heads(k_dir)), _both(vh), _both(rh), s0.astype(f32))
    o = o[0] + _rev(o[1])
    mean = jnp.mean(o, axis=-1, keepdims=True)
    var = jnp.mean(jnp.square(o - mean), axis=-1, keepdims=True)
    o = ((o - mean) * lax.rsqrt(var + GN_EPS)).reshape(bsz, t_len, A_WIDTH)
    o = o * gn_w.astype(f32) + gn_b.astype(f32)
    bonus = jnp.sum(rh * _heads(jnp.mean(k_dir, axis=0)) * r_k.astype(f32), axis=-1, keepdims=True) * vh
    y_a = (o + bonus.reshape(bsz, t_len, A_WIDTH)) * jax.nn.silu(gate_a)

    ug = u.reshape(bsz, t_len, B_GROUPS, B_GROUP_CH)
    y_b = jnp.fft.fft2(ug, axes=(1, 3), norm='ortho').real.reshape(bsz, t_len, B_WIDTH)
    y_b = y_b * jax.nn.silu(gate_b)
    y = jnp.concatenate([y_a, y_b], axis=-1).astype(h.dtype) @ w_out
    return y, s_fin


def _axial_rope(x):
    t_len = x.shape[1]
    rows = t_len // GRID_W
    row = jnp.repeat(jnp.arange(rows), GRID_W)
    col = jnp.tile(jnp.arange(GRID_W), rows)
    half = C_HEAD // 2
    nf = half // 2
    inv = 1.0 / (ROPE_BASE ** (jnp.arange(nf, dtype=jnp.float32) / nf))
    shape = (1, t_len) + (1,) * (x.ndim - 3) + (nf,)

    def rot(seg, pos):
        ang = pos.astype(jnp.float32)[:, None] * inv
        cos, sin = jnp.cos(ang).reshape(shape), jnp.sin(ang).reshape(shape)
        s1, s2 = seg[..., :nf], seg[..., nf:]
        return jnp.concatenate([s1 * cos - s2 * sin, s2 * cos + s1 * sin], axis=-1)

    x32 = x.astype(jnp.float32)
    return jnp.concatenate([rot(x32[..., :half], row), rot(x32[..., half:], col)], axis=-1).astype(x.dtype)


def _attend(q, k, v, valid, sink):
    s = jnp.einsum('bqhgd,bkhd->bhgqk', q, k).astype(jnp.float32) * (C_HEAD ** -0.5)
    if valid is not None:
        s = jnp.where(valid, s, NEG_INF)
    sink_col = jnp.broadcast_to(sink.astype(jnp.float32)[None, :, :, None, None], s.shape[:-1] + (1,))
    p = jax.nn.softmax(jnp.concatenate([s, sink_col], axis=-1), axis=-1)[..., :-1]
    return jnp.einsum('bhgqk,bkhd->bqhgd', p.astype(v.dtype), v)


def _attn_split(h, w_in):
    bsz, t_len, _ = h.shape
    proj = h @ w_in
    nq, nkv = C_HEADS * C_HEAD, C_KV_HEADS * C_HEAD
    q = proj[..., :nq].reshape(bsz, t_len, C_KV_HEADS, C_GROUP, C_HEAD)
    k = proj[..., nq:nq + nkv].reshape(bsz, t_len, C_KV_HEADS, C_HEAD)
    v = proj[..., nq + nkv:nq + 2 * nkv].reshape(bsz, t_len, C_KV_HEADS, C_HEAD)
    gate = proj[..., nq + 2 * nkv:]
    return q, k, v, gate


def _gated_out(o, gate, w_out):
    bsz, t_len = o.shape[:2]
    y = o.reshape(bsz, t_len, C_HEADS * C_HEAD).astype(jnp.float32) * jax.nn.silu(gate.astype(jnp.float32))
    return y.astype(gate.dtype) @ w_out


def _query_blocks(t):
    bsz, t_len = t.shape[:2]
    return jnp.moveaxis(t.reshape((bsz, t_len // BLOCK, BLOCK) + t.shape[2:]), 1, 0)


def _merge_blocks(o):
    nb, bsz = o.shape[:2]
    return jnp.moveaxis(o, 0, 1).reshape((bsz, nb * BLOCK) + o.shape[3:])


def _attention_context(h, w_in, sink, w_out):
    q, k, v, gate = _attn_split(h, w_in)
    sink = sink.reshape(C_KV_HEADS, C_GROUP)
    o = lax.map(lambda qb: _attend(qb, k, v, None, sink), _query_blocks(q))
    return _gated_out(_merge_blocks(o), gate, w_out), k, v


def _attention_latent(h, ctx_k, ctx_v, w_in, sink, w_out):
    q, k, v, gate = _attn_split(h, w_in)
    q, k = _axial_rope(q), _axial_rope(k)
    bsz, t_len = h.shape[:2]
    nb = t_len // BLOCK
    pad = ((0, 0), (BLOCK, BLOCK), (0, 0), (0, 0))
    kp = jnp.pad(k, pad).reshape(bsz, nb + 2, BLOCK, C_KV_HEADS, C_HEAD)
    vp = jnp.pad(v, pad).reshape(bsz, nb + 2, BLOCK, C_KV_HEADS, C_HEAD)

    def band(t):
        return jnp.moveaxis(jnp.concatenate([t[:, :-2], t[:, 1:-1], t[:, 2:]], axis=2), 1, 0)

    qpos = jnp.arange(nb)[:, None, None] * BLOCK + jnp.arange(BLOCK)[None, :, None]
    kpos = jnp.arange(nb)[:, None, None] * BLOCK - BLOCK + jnp.arange(3 * BLOCK)[None, None, :]
    valid = (jnp.abs(qpos - kpos) <= WINDOW) & (kpos >= 0) & (kpos < t_len)
    valid = jnp.concatenate([valid, jnp.ones((nb, BLOCK, ctx_k.shape[1]), dtype=bool)], axis=-1)
    sink = sink.reshape(C_KV_HEADS, C_GROUP)

    def blk(args):
        qi, ki, vi, mi = args
        return _attend(qi, jnp.concatenate([ki, ctx_k], axis=1), jnp.concatenate([vi, ctx_v], axis=1), mi, sink)

    o = lax.map(blk, (_query_blocks(q), band(kp), band(vp), valid))
    return _gated_out(_merge_blocks(o), gate, w_out)


def setup_inputs(seed: int = 0) -> dict:
    key = jax.random.key(seed)
    ks = iter(jax.random.split(key, 32))
    f32 = jnp.float32

    def nrm(shape, scale):
        return jax.random.normal(next(ks), shape, f32) * scale

    def unif(shape, lo, hi):
        return jax.random.uniform(next(ks), shape, f32, lo, hi)

    d = D_MODEL
    return {
        'x_prompt': nrm((BATCH, SEQ, d), 1.0),
        'x_sample': nrm((DEC_BATCH, DEC_SEQ, d), 1.0),
        'c': nrm((DEC_BATCH, d), 1.0),
        'state_wkv': nrm((DEC_BATCH, N_EVEN, 2, A_HEADS, A_HEAD, A_HEAD), 0.5),
        'cache_k': nrm((DEC_BATCH, N_ODD, PAST_LEN, C_KV_HEADS, C_HEAD), 1.0),
        'cache_v': nrm((DEC_BATCH, N_ODD, PAST_LEN, C_KV_HEADS, C_HEAD), 1.0),
        'c_ctx': nrm((d,), 1.0),
        'mod_w': nrm((DEPTH, d, 3 * d), 0.5 * d ** -0.5),
        'mod_b': nrm((DEPTH, 3 * d), 0.01),
        'norm_pre': 1.0 + nrm((DEPTH, d), 0.05),
        'norm_post': 1.0 + nrm((DEPTH, d), 0.05),
        'even_w_in': nrm((N_EVEN, d, EVEN_IN), d ** -0.5),
        'even_mu': unif((N_EVEN, SHIFT_COLS), 0.0, 1.0),
        'even_w0': unif((N_EVEN, 2, A_WIDTH), -4.0, 1.0),
        'even_w_up': nrm((N_EVEN, 2, LORA, A_WIDTH), 0.5 * LORA ** -0.5),
        'even_a0': nrm((N_EVEN, 2, A_WIDTH), 0.5),
        'even_a_up': nrm((N_EVEN, 2, LORA, A_WIDTH), 0.5 * LORA ** -0.5),
        'even_k_k': 0.85 + nrm((N_EVEN, A_WIDTH), 0.05),
        'even_k_a': 1.0 + nrm((N_EVEN, A_WIDTH), 0.05),
        'even_r_k': nrm((N_EVEN, A_HEADS, A_HEAD), 0.1),
        'even_gn_w': 1.0 + nrm((N_EVEN, A_WIDTH), 0.05),
        'even_gn_b': nrm((N_EVEN, A_WIDTH), 0.01),
        'even_w_out': nrm((N_EVEN, d, d), d ** -0.5),
        'odd_w_in': nrm((N_ODD, d, ODD_IN), d ** -0.5),
        'odd_sink': nrm((N_ODD, C_HEADS), 1.0),
        'odd_w_out': nrm((N_ODD, d, d), d ** -0.5),
    }


def reference(x_prompt, x_sample, c, state_wkv, cache_k, cache_v, c_ctx, mod_w, mod_b, norm_pre, norm_post,
              even_w_in, even_mu, even_w0, even_w_up, even_a0, even_a_up, even_k_k, even_k_a, even_r_k,
              even_gn_w, even_gn_b, even_w_out, odd_w_in, odd_sink, odd_w_out):
    xp, xs = x_prompt, x_sample
    new_wkv, new_k, new_v = [], [], []
    for layer in range(DEPTH):
        i = layer // 2
        sh_p, sc_p, g_p = _modulation(c_ctx, mod_w[layer], mod_b[layer])
        sh_s, sc_s, g_s = _modulation(c, mod_w[layer], mod_b[layer])
        hp = _rmsnorm(xp, norm_pre[layer]) * (1.0 + sc_p) + sh_p
        hs = _rmsnorm(xs, norm_pre[layer]) * (1.0 + sc_s) + sh_s
        if layer % 2 == 0:
            p = (even_w_in[i], even_mu[i], even_w0[i], even_w_up[i], even_a0[i], even_a_up[i],
                 even_k_k[i], even_k_a[i], even_r_k[i], even_gn_w[i], even_gn_b[i], even_w_out[i])
            s0 = jnp.zeros((2, xp.shape[0], A_HEADS, A_HEAD, A_HEAD), jnp.float32)
            yp, s_ctx = _rwkv_fourier_mixer(hp, s0, *p)
            ys, _ = _rwkv_fourier_mixer(hs, jnp.moveaxis(state_wkv[:, i], 1, 0), *p)
            new_wkv.append(jnp.moveaxis(s_ctx, 0, 1))
        else:
            yp, kc, vc = _attention_context(hp, odd_w_in[i], odd_sink[i], odd_w_out[i])
            ys = _attention_latent(hs, cache_k[:, i], cache_v[:, i], odd_w_in[i], odd_sink[i], odd_w_out[i])
            new_k.append(kc)
            new_v.append(vc)
        xp = xp + g_p * _rmsnorm(yp, norm_post[layer])
        xs = xs + g_s * _rmsnorm(ys, norm_post[layer])
    new_state_wkv = jnp.stack(new_wkv, axis=1)
    new_cache_k = jnp.stack(new_k, axis=1)
    new_cache_v = jnp.stack(new_v, axis=1)
    return (xp, xs, new_state_wkv, new_cache_k, new_cache_v)
```

```python
import math
from contextlib import ExitStack
import numpy as np
import ml_dtypes
import concourse.bass as bass
import concourse.mybir as mybir
from concourse.bass_utils import run_bass_kernel_spmd

F32 = mybir.dt.float32
BF16 = mybir.dt.bfloat16
AF = mybir.ActivationFunctionType
ALU = mybir.AluOpType
AX = mybir.AxisListType
NPBF = ml_dtypes.bfloat16


class Cfg:
    def __init__(self, D=2048, DEPTH=4, TS=2048, TP=256, NP=4, PAST=256, BG=4, GRID_W=64):
        self.D = D
        self.DEPTH = DEPTH
        self.TS = TS
        self.TP = TP
        self.NP = NP
        self.PAST = PAST
        self.GRID_W = GRID_W
        self.NT = TS + NP * TP
        self.KD = D // 128
        self.NE = (DEPTH + 1) // 2
        self.NO = DEPTH // 2
        self.AW = D // 2
        self.AH = self.AW // 64
        self.LORA = 64
        self.BW = D - self.AW
        self.BG = BG
        self.BGC = self.BW // BG
        self.SHIFT = 3 * self.AW + 4 * self.LORA
        self.EIN = self.SHIFT + self.AW + 2 * self.BW
        self.CH = D // 64
        self.CKV = self.CH // 4
        self.OIN = (self.CH + 2 * self.CKV) * 64 + D
        self.seqs = [(0, TS, 1)] + [(TS + i * TP, TP, 0) for i in range(NP)]
        self.NTP = self.NT + len(self.seqs) + 1
        self.NCH = self.NT // 128


class StopBuild(Exception):
    pass


class Buf:
    __slots__ = ("wb", "rb", "name", "bw", "nb")

    def __init__(self, name="", span=1 << 20, nb=1):
        self.name = name
        self.nb = max(1, nb)
        self.bw = max(1, (span + self.nb - 1) // self.nb)
        self.wb = [[] for _ in range(self.nb)]
        self.rb = [[] for _ in range(self.nb)]

    def buckets(self, box):
        b0 = min(self.nb - 1, box[2] // self.bw)
        b1 = min(self.nb - 1, (box[3] - 1) // self.bw)
        for b in range(b0, b1 + 1):
            lo = max(box[2], b * self.bw) if b < self.nb - 1 or True else box[2]
            hi = min(box[3], (b + 1) * self.bw) if b < self.nb - 1 else box[3]
            yield b, (box[0], box[1], lo, hi)


def _ovl(a, b):
    return a[0] < b[1] and b[0] < a[1] and a[2] < b[3] and b[2] < a[3]


def _contains(a, b):
    return a[0] <= b[0] and a[1] >= b[1] and a[2] <= b[2] and a[3] >= b[3]


def _union(a, b):
    return (min(a[0], b[0]), max(a[1], b[1]), min(a[2], b[2]), max(a[3], b[3]))


def _coarsen(lst):
    d = {}
    for box, s, v in lst:
        if s in d:
            d[s] = (_union(d[s][0], box), max(d[s][1], v))
        else:
            d[s] = (box, v)
    return [(bx, s, v) for s, (bx, v) in d.items()]


MAXLIST = 48


ENGS = ("pe", "act", "dve", "pool", "sp")
EPOCH = 30000
NDMA = 12


class Prog:
    def __init__(self, nc, stack):
        self.nc = nc
        self.stack = stack
        self.ops = {e: [] for e in ENGS}
        self.cnt = {e: 0 for e in ENGS}
        self.esem = {e: [] for e in ENGS}
        self.known = {e: {} for e in ENGS}
        self.semobj = {}
        self.dn = {e: 0 for e in ENGS}
        self.dsem = {e: [] for e in ENGS}
        self.nsem = 0

    def newsem(self, name):
        s = self.stack.enter_context(self.nc.semaphore(name))
        self.nsem += 1
        self.semobj[name] = s
        return name

    def _ticket(self, eng):
        i = self.cnt[eng]
        self.cnt[eng] += 1
        ep = i // EPOCH
        while len(self.esem[eng]) <= ep:
            self.esem[eng].append(self.newsem(f"e_{eng}_{len(self.esem[eng])}"))
        return (self.esem[eng][ep], i % EPOCH + 1)

    def _waits(self, eng, deps):
        kn = self.known[eng]
        out = []
        for s, v in deps.items():
            if kn.get(s, 0) < v:
                kn[s] = v
                out.append((s, v))
        return out

    @staticmethod
    def _merge(d, s, v):
        if d.get(s, 0) < v:
            d[s] = v

    def _deps(self, reads, writes):
        deps = {}
        mg = self._merge
        for b, box in reads:
            for bi, cb in b.buckets(box):
                for bx, s, v in b.wb[bi]:
                    if _ovl(bx, cb):
                        mg(deps, s, v)
        for b, box in writes:
            for bi, cb in b.buckets(box):
                for bx, s, v in b.wb[bi]:
                    if _ovl(bx, cb):
                        mg(deps, s, v)
                for bx, s, v in b.rb[bi]:
                    if _ovl(bx, cb):
                        mg(deps, s, v)
        return deps

    def _commit(self, t, reads, writes):
        ts, tv = t
        for b, box in reads:
            for bi, cb in b.buckets(box):
                lst = b.rb[bi]
                done = False
                for i, (bx, s, v) in enumerate(lst):
                    if s == ts and _contains(bx, cb):
                        if v < tv:
                            lst[i] = (bx, s, tv)
                        done = True
                        break
                if not done:
                    lst = [e for e in lst if not (e[1] == ts and _contains(cb, e[0]))]
                    lst.append((cb, ts, tv))
                    if len(lst) > MAXLIST:
                        lst = _coarsen(lst)
                    b.rb[bi] = lst
        for b, box in writes:
            for bi, cb in b.buckets(box):
                lw = [e for e in b.wb[bi] if not _contains(cb, e[0])]
                lw.append((cb, ts, tv))
                if len(lw) > MAXLIST:
                    lw = _coarsen(lw)
                b.wb[bi] = lw
                b.rb[bi] = [e for e in b.rb[bi] if not _contains(cb, e[0])]

    def op(self, eng, fn, reads=(), writes=()):
        deps = self._deps(reads, writes)
        waits = self._waits(eng, deps)
        t = self._ticket(eng)
        self.ops[eng].append((waits, fn, (t[0], 1)))
        self._commit(t, reads, writes)
        return t

    def dma(self, eng, out, in_, reads=(), writes=(), **kw):
        deps = self._deps(reads, writes)
        n = self.dn[eng]
        self.dn[eng] += 1
        slot = n % NDMA
        while len(self.dsem[eng]) <= slot:
            self.dsem[eng].append(self.newsem(f"d_{eng}_{len(self.dsem[eng])}"))
        s = self.dsem[eng][slot]
        use = n // NDMA
        if use > 0:
            self._merge(deps, s, 16 * use)
        waits = self._waits(eng, deps)
        t = (s, 16 * (use + 1))

        def fn(e, out=out, in_=in_, kw=kw):
            return e.dma_start(out=out, in_=in_, **kw)

        self.ops[eng].append((waits, fn, (s, 16)))
        self._commit(t, reads, writes)
        return t

    def barrier(self):
        deps = {}
        for e in ENGS:
            i = self.cnt[e]
            if i > 0:
                ep = (i - 1) // EPOCH
                self._merge(deps, self.esem[e][ep], (i - 1) % EPOCH + 1)
            n = self.dn[e]
            for slot, s in enumerate(self.dsem[e]):
                uses = (n - slot + NDMA - 1) // NDMA
                if uses > 0:
                    self._merge(deps, s, 16 * uses)
        for e in ENGS:
            waits = self._waits(e, dict(deps))
            if waits:
                self.ops[e].append((waits, None, None))

    def emit(self):
        nc = self.nc
        so = self.semobj
        with nc.Block() as block:
            def run(e, lst):
                for waits, fn, inc in lst:
                    for s, v in waits:
                        e.wait_ge(so[s], v)
                    if fn is not None:
                        ins = fn(e)
                        ins.then_inc(so[inc[0]], inc[1])

            @block.tensor
            def _(e):
                run(e, self.ops["pe"])

            @block.scalar
            def _(e):
                run(e, self.ops["act"])

            @block.vector
            def _(e):
                run(e, self.ops["dve"])

            @block.gpsimd
            def _(e):
                run(e, self.ops["pool"])

            @block.sync
            def _(e):
                run(e, self.ops["sp"])


class Arena:
    def __init__(self, t, nbytes):
        self.t = t
        self.n = nbytes
        self.p = 0
        self.marks = []

    def alloc(self, shape, dtype=F32):
        sz = 4 if dtype == F32 else 2
        cols = int(np.prod(shape))
        nb = (cols * sz + 63) // 64 * 64
        assert self.p + nb <= self.n, f"arena overflow {self.p}+{nb}>{self.n}"
        w0 = self.p // 4
        w1 = (self.p + nb) // 4
        self.p += nb
        ap = self.t[:, w0:w1]
        if dtype != F32:
            ap = ap.bitcast(dtype)
        ap = ap[:, 0:cols]
        if len(shape) == 2:
            ap = ap.rearrange("p (a b) -> p a b", a=shape[0])
        elif len(shape) == 3:
            ap = ap.rearrange("p (a b c) -> p a b c", a=shape[0], b=shape[1])
        return ap

    def mark(self):
        self.marks.append(self.p)

    def release(self):
        self.p = self.marks.pop()


_DT_SZ = {F32: 4, BF16: 2}


class T:
    __slots__ = ("ap", "buf", "dram")

    def __init__(self, ap, buf=None, name="", dram=False):
        self.ap = ap
        if buf is None:
            nbytes = _DT_SZ.get(ap.dtype, 4)
            for d_ in (ap.tensor.shape if dram else ap.tensor.shape[1:]):
                nbytes *= d_
            buf = Buf(name, span=nbytes, nb=(128 if dram else 1))
        self.buf = buf
        self.dram = dram

    def __getitem__(self, key):
        return T(self.ap[key], self.buf, dram=self.dram)

    def re(self, s, **kw):
        return T(self.ap.rearrange(s, **kw), self.buf, dram=self.dram)

    def bc(self, dt):
        return T(self.ap.bitcast(dt), self.buf, dram=self.dram)

    def bcast(self, shape):
        return T(self.ap.broadcast_to(shape), self.buf, dram=self.dram)

    def box(self):
        ap = self.ap
        sz = _DT_SZ.get(ap.dtype, 4)
        dims = ap.ap
        off = ap.offset
        if self.dram:
            lo = hi = off
            for st, cn in dims:
                if st < 0:
                    lo += st * (cn - 1)
                else:
                    hi += st * (cn - 1)
            return (0, 1, lo * sz, (hi + 1) * sz)
        row = 1
        for d_ in ap.tensor.shape[1:]:
            row *= d_
        p0 = off // row
        col = off % row
        pst, pcn = dims[0]
        p1 = p0 + (pcn - 1) * max(1, pst // row) + 1
        lo = hi = col
        for st, cn in dims[1:]:
            if st < 0:
                lo += st * (cn - 1)
            else:
                hi += st * (cn - 1)
        return (p0, p1, lo * sz, (hi + 1) * sz)


def _ap(x):
    return x.ap if isinstance(x, T) else x


def _bufs(*xs):
    return [(x.buf, x.box()) for x in xs if isinstance(x, T)]


class K:
    def __init__(self, P):
        self.P = P

    def mm(self, out, pairs, extra_reads=()):
        aps = [(_ap(l), _ap(r)) for l, r in pairs]
        o = _ap(out)
        n = len(aps)

        def fn(e):
            ins = None
            for i, (l, r) in enumerate(aps):
                ins = e.matmul(o, lhsT=l, rhs=r, start=(i == 0), stop=(i == n - 1))
            return ins

        reads = []
        for l, r in pairs:
            reads += _bufs(l, r)
        return self.P.op("pe", fn, reads=list(reads) + list(extra_reads), writes=_bufs(out))

    def mms(self, groups):
        gl = [(_ap(o), [(_ap(l), _ap(r)) for l, r in pairs]) for o, pairs in groups]

        def fn(e):
            ins = None
            for o, aps in gl:
                n = len(aps)
                for i, (l, r) in enumerate(aps):
                    ins = e.matmul(o, lhsT=l, rhs=r, start=(i == 0), stop=(i == n - 1))
            return ins

        reads, pw = [], []
        for o, pairs in groups:
            pw += _bufs(o)
            for l, r in pairs:
                reads += _bufs(l, r)
        return self.P.op("pe", fn, reads=reads, writes=pw)

    def act(self, out, in_, func, bias=None, scale=None, accum=None, eng="act"):
        kw = {}
        if bias is not None:
            kw["bias"] = _ap(bias)
        if scale is not None:
            kw["scale"] = _ap(scale)
        if accum is not None:
            kw["accum_out"] = _ap(accum)
        o, i = _ap(out), _ap(in_)
        return self.P.op(eng, lambda e: e.activation(out=o, in_=i, func=func, **kw),
                         reads=_bufs(in_, bias, scale), writes=_bufs(out, accum))

    def tt(self, eng, out, a, b, op):
        o, x, y = _ap(out), _ap(a), _ap(b)
        return self.P.op(eng, lambda e: e.tensor_tensor(o, x, y, op), reads=_bufs(a, b), writes=_bufs(out))

    def ts(self, eng, out, a, s1, op0, s2=None, op1=None, accum=None):
        o, x = _ap(out), _ap(a)
        v1, v2 = _ap(s1), _ap(s2)
        kw = {}
        if op1 is not None:
            kw["op1"] = op1
        if accum is not None:
            kw["accum_out"] = _ap(accum)
        return self.P.op(eng, lambda e: e.tensor_scalar(o, x, v1, v2, op0, **kw),
                         reads=_bufs(a, s1, s2), writes=_bufs(out, accum))

    def stt(self, eng, out, a, s, b, op0, op1):
        o, x, y, v = _ap(out), _ap(a), _ap(b), _ap(s)
        return self.P.op(eng, lambda e: e.scalar_tensor_tensor(o, x, v, y, op0, op1),
                         reads=_bufs(a, s, b), writes=_bufs(out))

    def copy(self, eng, out, a):
        o, x = _ap(out), _ap(a)
        if eng == "act":
            return self.P.op(eng, lambda e: e.copy(o, x), reads=_bufs(a), writes=_bufs(out))
        return self.P.op(eng, lambda e: e.tensor_copy(o, x), reads=_bufs(a), writes=_bufs(out))

    def memset(self, eng, out, val):
        o = _ap(out)
        return self.P.op(eng, lambda e: e.memset(o, val), writes=_bufs(out))

    def recip(self, out, a):
        o, x = _ap(out), _ap(a)
        return self.P.op("dve", lambda e: e.reciprocal(o, x), reads=_bufs(a), writes=_bufs(out))

    def scan(self, out, d0, d1, init, op0, op1):
        o, x, y = _ap(out), _ap(d0), _ap(d1)
        return self.P.op("dve", lambda e: e.tensor_tensor_scan(o, x, y, init, op0, op1),
                         reads=_bufs(d0, d1), writes=_bufs(out))

    def dma(self, eng, out, in_, **kw):
        return self.P.dma(eng, _ap(out), _ap(in_), reads=_bufs(in_), writes=_bufs(out), **kw)


LAM = math.exp(-0.5)


def host_consts(cfg):
    c = {}
    c["identb"] = np.eye(128, dtype=np.float32).astype(NPBF)
    c["identf"] = np.eye(128, dtype=np.float32)
    sel = np.zeros((2, 256), np.float32)
    sel[0, 0:128] = 1.0
    sel[1, 128:256] = 1.0
    c["sel"] = sel
    bo = np.zeros((128, 128), np.float32)
    bo[0:64, 0:64] = 1.0
    bo[64:128, 64:128] = 1.0
    c["bones"] = bo.astype(NPBF)
    c["onesb"] = np.ones((128, 64), np.float32).astype(NPBF)
    mf = np.ones((cfg.NTP,), np.float32)
    mb = np.ones((cfg.NTP,), np.float32)
    for si, (off, T, _) in enumerate(cfg.seqs):
        pc = off + si + 1
        for n in range(T // 128):
            mf[pc + n * 128] = 0.0
            mb[pc + n * 128 + 127] = 0.0
    c["rmask"] = np.stack([np.broadcast_to(mf, (128, cfg.NTP)), np.broadcast_to(mb, (128, cfg.NTP))], 1).copy()
    r = np.arange(128)[:, None]
    q = np.arange(128)[None, :]
    def rep(m):
        return np.broadcast_to(m.astype(np.float32)[:, None, :], (128, 4, 128)).copy()
    c["m_lt"] = rep(q < r)
    c["m_gt"] = rep(q > r)
    c["m_ge"] = rep(q >= r)
    c["m_le"] = rep(q <= r)
    bd = ((r < 64) == (q < 64))
    c["m_lt_bd"] = rep((q < r) & bd)
    c["m_gt_bd"] = rep((q > r) & bd)
    c["m_ll"] = rep((r >= 64) & (q < 64))
    c["m_ur"] = rep((r < 64) & (q >= 64))
    def dft(n, scale):
        idx = (np.arange(n)[:, None] * np.arange(n)[None, :]) % n
        ang = 2.0 * np.pi * idx.astype(np.float64) / n
        return (np.cos(ang) * scale), (np.sin(ang) * scale)
    cc, sc = dft(cfg.BGC, cfg.BGC ** -0.5)
    c["dftc"] = np.concatenate([cc, sc], 1).astype(np.float32).astype(NPBF)
    for T in sorted({cfg.TS, cfg.TP}):
        ct, st = dft(T, T ** -0.5)
        c[f"dct{T}"] = ct.astype(np.float32).astype(NPBF)
        c[f"dst{T}"] = (-st).astype(np.float32).astype(NPBF)
    half, nf = 32, 16
    inv = 1.0 / (10000.0 ** (np.arange(nf, dtype=np.float32) / nf))
    pos = np.arange(cfg.TS)
    row = (pos // cfg.GRID_W).astype(np.float32)
    col = (pos % cfg.GRID_W).astype(np.float32)
    cosT = np.zeros((64, cfg.TS), np.float32)
    sinT = np.zeros((64, cfg.TS), np.float32)
    for d in range(64):
        p = row if d < half else col
        dd = d % half
        f = dd % nf
        ang = p * inv[f]
        cosT[d] = np.cos(ang)
        sinT[d] = np.sin(ang) * (-1.0 if dd < nf else 1.0)
    c["ropec"] = np.concatenate([cosT, cosT], 0)
    c["ropes"] = np.concatenate([sinT, sinT], 0)
    jk = np.arange(128)[:, None]
    iq = np.arange(128)[None, :]
    mp = np.where(jk >= iq, 1.0, 0.0)
    mn = np.where(jk <= iq, 1.0, 0.0)
    c["amaskp"] = np.broadcast_to(mp[:, None, :], (128, 4, 128)).astype(np.float32).astype(NPBF).copy()
    c["amaskn"] = np.broadcast_to(mn[:, None, :], (128, 4, 128)).astype(np.float32).astype(NPBF).copy()
    return c


CONST_DT = {"identb": BF16, "bones": BF16, "onesb": BF16, "dftc": BF16, "amaskp": BF16, "amaskn": BF16}


def build(cfg, debug=(), stop_after=None, sbuf_kb=206):
    nc = bass.Bass("TRN2", target_bir_lowering=False)
    stack = ExitStack()
    P = Prog(nc, stack)
    k = K(P)
    D, KD, NT, NTP = cfg.D, cfg.KD, cfg.NT, cfg.NTP
    NE, NO, AW, BW = cfg.NE, cfg.NO, cfg.AW, cfg.BW

    def din(name, shape, dt=F32):
        return T(nc.dram_tensor(name, list(shape), dt, kind="ExternalInput").ap(), name=name, dram=True)

    def dout(name, shape, dt=F32):
        return T(nc.dram_tensor(name, list(shape), dt, kind="ExternalOutput").ap(), name=name, dram=True)

    def dscr(name, shape, dt=F32):
        kind = "ExternalOutput" if name in debug else "Internal"
        return T(nc.dram_tensor(name, list(shape), dt, kind=kind).ap(), name=name, dram=True)

    x_in = din("x", [NT, D])
    cond_in = din("cond", [128, KD, 2])
    state_in = din("state", [NE, 2, cfg.AH, 64, 64])
    ck_in = din("cache_k", [NO, cfg.PAST, cfg.CKV * 64])
    cv_in = din("cache_v", [NO, cfg.PAST, cfg.CKV * 64])
    mod_w = din("mod_w", [cfg.DEPTH, D, 3 * D])
    mod_b = din("mod_b", [cfg.DEPTH, 3 * D])
    npre = din("norm_pre", [cfg.DEPTH, D])
    npost = din("norm_post", [cfg.DEPTH, D])
    e_w_in = din("even_w_in", [NE, D, cfg.EIN])
    e_mu = din("even_mu", [NE, 128, cfg.SHIFT // 128])
    e_w0 = din("even_w0", [NE, 128, 2, AW // 128])
    e_w_up = din("even_w_up", [NE, 2, 64, AW])
    e_a0 = din("even_a0", [NE, 128, 2, AW // 128])
    e_a_up = din("even_a_up", [NE, 2, 64, AW])
    e_vec = din("even_vec", [NE, 128, 5, AW // 128])
    e_w_out = din("even_w_out", [NE, D, D])
    o_w_in = din("odd_w_in", [NO, D, cfg.OIN])
    o_sink = din("odd_sink", [NO, 1, cfg.CH])
    o_w_out = din("odd_w_out", [NO, D, D])
    consts = host_consts(cfg)
    cin = {n: din("c_" + n, v.shape, CONST_DT.get(n, BF16 if v.dtype == NPBF else F32)) for n, v in consts.items()}

    y_out = dout("y", [NT, D])
    st_out = dout("new_state", [cfg.NP, NE, 2, cfg.AH, 64, 64])
    nk_out = dout("new_k", [cfg.NP, NO, cfg.TP, cfg.CKV * 64])
    nv_out = dout("new_v", [cfg.NP, NO, cfg.TP, cfg.CKV * 64])

    XR = dscr("XR", [NT, D])
    PT = dscr("PT", [max(cfg.EIN, cfg.OIN), NTP])
    YT = dscr("YT", [D, NTP], BF16)

    big = stack.enter_context(nc.sbuf_tensor("big", [128, sbuf_kb * 256], F32))
    A = Arena(big, sbuf_kb * 1024)

    bigbuf = Buf("big", span=sbuf_kb * 1024, nb=sbuf_kb * 2)

    def alloc(shape, dt=F32, name=""):
        return T(A.alloc(shape, dt), bigbuf)

    ps = [T(stack.enter_context(nc.psum_tensor(f"ps{i}", [128, 512], F32))[:, :], name=f"ps{i}") for i in range(8)]

    identb = alloc([128], BF16)
    identf = alloc([128])
    sel = alloc([256])
    bones = alloc([128], BF16)
    onesb = alloc([64], BF16)
    k.dma("sp", identb, cin["identb"])
    k.dma("sp", identf, cin["identf"])
    k.dma("sp", sel[0:2], cin["sel"])
    k.dma("sp", bones, cin["bones"])
    k.dma("sp", onesb, cin["onesb"])
    condt = alloc([KD, 2])
    sT = alloc([KD, 2], BF16)
    k.dma("sp", condt, cond_in)
    k.act(sT, condt, AF.Silu)
    gsF = alloc([KD, 2])
    shF = alloc([KD, 2])
    CG = [alloc([D]), alloc([D])]
    HTbox = [None]

    blocks = []
    for si, (off, Tn, ci) in enumerate(cfg.seqs):
        for b0 in range(0, Tn, 512):
            sz = min(512, Tn - b0)
            blocks.append((si, off + b0, sz, off + b0 + si + 1))
    tiles = []
    for si, (off, Tn, ci) in enumerate(cfg.seqs):
        for t0 in range(0, Tn, 128):
            tiles.append(((off + t0) // 128, ci, si))

    ev_rr = [0]

    def evac(out, in_):
        ev_rr[0] ^= 1
        if ev_rr[0]:
            k.act(out, in_, AF.Identity)
        else:
            k.copy("dve", out, in_)

    def phase_mod(l):
        A.mark()
        m = alloc([3 * D])
        mb = alloc([3 * D])
        np2 = alloc([D])
        nq2 = alloc([D])
        gs2 = alloc([D])
        cg2 = alloc([D])
        wt = [alloc([KD, 512], BF16) for _ in range(2)]
        for p in range(2):
            k.dma("sp", mb[p:p + 1], mod_b[l:l + 1, :])
            k.dma("sp", np2[p:p + 1], npre[l:l + 1, :])
            k.dma("sp", nq2[p:p + 1], npost[l:l + 1, :])
        nj = 3 * D // 512
        for j in range(nj):
            w = wt[j % 2]
            k.dma("pool", w, mod_w[l, :, j * 512:(j + 1) * 512].re("(k p) c -> p k c", p=128))
            pb = ps[j % 2]
            k.mm(pb[0:2, :], [(sT[:, kk, :], w[:, kk, :]) for kk in range(KD)])
            k.tt("dve", m[0:2, j * 512:(j + 1) * 512], pb[0:2, :], mb[0:2, j * 512:(j + 1) * 512], ALU.add)
        k.stt("dve", gs2[0:2], m[0:2, D:2 * D], 1.0, np2[0:2], ALU.add, ALU.mult)
        k.tt("dve", cg2[0:2], m[0:2, 2 * D:3 * D], nq2[0:2], ALU.mult)
        i2 = identf[0:2, 0:2]
        k.mms([(ps[2][:, 2 * kk:2 * kk + 2], [(gs2[0:2, kk * 128:(kk + 1) * 128], i2)]) for kk in range(KD)])
        k.mms([(ps[3][:, 2 * kk:2 * kk + 2], [(m[0:2, kk * 128:(kk + 1) * 128], i2)]) for kk in range(KD)])
        k.copy("dve", gsF.re("p k c -> p (k c)"), ps[2][:, 0:2 * KD])
        k.copy("dve", shF.re("p k c -> p (k c)"), ps[3][:, 0:2 * KD])
        for c in range(2):
            for j in range(D // 512):
                pb = ps[4 + (j % 2)]
                k.mm(pb[:, :], [(sel[0:2, c * 128:(c + 1) * 128], cg2[0:2, j * 512:(j + 1) * 512])])
                evac(CG[c][:, j * 512:(j + 1) * 512], pb[:, :])
        A.release()

    def phase_norm(l, src):
        A.mark()
        xb = [alloc([D]) for _ in range(2)]
        xn = [alloc([D], BF16) for _ in range(2)]
        junk = alloc([D], BF16)
        ss = alloc([cfg.NCH])
        rs = alloc([cfg.NCH])
        k.memset("dve", ss, 0.0)
        for n, (ti, ci, si) in enumerate(tiles):
            xt = xb[n % 2]
            k.dma("sp", xt, src[ti * 128:(ti + 1) * 128, :])
            k.act(junk, xt, AF.Square, accum=ss[:, ti:ti + 1])
            k.ts("dve", rs[:, ti:ti + 1], ss[:, ti:ti + 1], 1.0 / D, ALU.mult, 1e-6, ALU.add)
            k.act(rs[:, ti:ti + 1], rs[:, ti:ti + 1], AF.Sqrt)
            k.recip(rs[:, ti:ti + 1], rs[:, ti:ti + 1])
            xq = xn[n % 2]
            k.act(xq, xt, AF.Copy, scale=rs[:, ti:ti + 1])
            for kq in range(KD // 4):
                pb = ps[(n * (KD // 4) + kq) % 8]
                k.mms([(pb[:, j * 128:(j + 1) * 128], [(xq[:, (kq * 4 + j) * 128:(kq * 4 + j + 1) * 128], identb)])
                       for j in range(4)])
                for j in range(4):
                    kk = kq * 4 + j
                    k.act(HTbox[0][:, kk, ti * 128:(ti + 1) * 128], pb[:, j * 128:(j + 1) * 128], AF.Identity,
                          scale=gsF[:, kk, ci:ci + 1], bias=shF[:, kk, ci:ci + 1])
        A.release()

    def phase_inproj(W, ncol, nshift, mu):
        A.mark()
        NC4 = 2
        wt = [alloc([KD, 128 * NC4], BF16) for _ in range(2)]
        rows = [alloc([NTP]) for _ in range(2)]
        rb = alloc([NTP])
        ro = [alloc([NTP]) for _ in range(2)]
        for r_ in rows + [rb] + ro:
            k.memset("pool", r_, 0.0)
        if nshift:
            mut = alloc([nshift // 128])
            amu = alloc([nshift // 128])
            bmu = alloc([nshift // 128])
            k.dma("sp", mut, mu)
            k.ts("dve", amu, mut, -1.0, ALU.mult, 1.0, ALU.add)
            k.ts("dve", bmu, mut, 0.5, ALU.mult)
        nchunk = ncol // 128
        pbi = 0
        for c in range(nchunk):
            if c % NC4 == 0:
                w = wt[(c // NC4) % 2]
                wc = min(128 * NC4, ncol - c * 128)
                k.dma("pool", w[:, :, 0:wc], W[:, c * 128:c * 128 + wc].re("(k p) c -> p k c", p=128))
            cc = c % NC4
            row = rows[c % 2]
            for (si, t0, sz, pc) in blocks:
                pb = ps[pbi % 8]
                pbi += 1
                k.mm(pb[:, 0:sz], [(w[:, kk, cc * 128:(cc + 1) * 128], HTbox[0][:, kk, t0:t0 + sz]) for kk in range(KD)])
                evac(row[:, pc:pc + sz], pb[:, 0:sz])
            if c < nshift // 128:
                o = ro[c % 2]
                k.act(o[:, 1:NTP - 1], row[:, 1:NTP - 1], AF.Copy, scale=amu[:, c:c + 1])
                k.tt("pool", rb[:, 1:NTP - 1], row[:, 0:NTP - 2], row[:, 2:NTP], ALU.add)
                k.stt("dve", o[:, 1:NTP - 1], rb[:, 1:NTP - 1], bmu[:, c:c + 1], o[:, 1:NTP - 1], ALU.mult, ALU.add)
                k.dma("sp", PT[c * 128:(c + 1) * 128, :], o)
            else:
                k.dma("sp", PT[c * 128:(c + 1) * 128, :], row)
        A.release()

    def phase_outproj(l, Wout, src, dst):
        A.mark()
        wo = alloc([KD, D], BF16)
        for q in range(D // 512):
            k.dma("pool", wo[:, :, q * 512:(q + 1) * 512], Wout[:, q * 512:(q + 1) * 512].re("(k p) c -> p k c", p=128))
        yb = [alloc([KD, 512], BF16) for _ in range(2)]
        ysb = [alloc([D]) for _ in range(2)]
        xb = [alloc([D]) for _ in range(2)]
        tmp = [alloc([D]) for _ in range(2)]
        junk = alloc([D], BF16)
        ss = alloc([cfg.NCH])
        rs = alloc([cfg.NCH])
        k.memset("dve", ss, 0.0)
        nb = D // 512
        n = 0
        for bi, (si, t0, sz, pc) in enumerate(blocks):
            ci = cfg.seqs[si][2]
            ybt = yb[bi % 2]
            k.dma("sp", ybt[:, :, 0:sz], src[:, pc:pc + sz].re("(k p) t -> p k t", p=128))
            for j in range(sz // 128):
                ti = (t0 + j * 128) // 128
                yt = ysb[n % 2]
                xt = xb[n % 2]
                tm = tmp[n % 2]
                k.dma("act", xt, XRsrc[0][ti * 128:(ti + 1) * 128, :])
                for q in range(nb):
                    pb = ps[(n % 2) * 4 + (q % 4)]
                    k.mm(pb[:, :], [(ybt[:, kk, j * 128:(j + 1) * 128], wo[:, kk, q * 512:(q + 1) * 512]) for kk in range(KD)])
                    evac(yt[:, q * 512:(q + 1) * 512], pb[:, :])
                k.act(junk, yt, AF.Square, accum=ss[:, ti:ti + 1])
                k.ts("dve", rs[:, ti:ti + 1], ss[:, ti:ti + 1], 1.0 / D, ALU.mult, 1e-6, ALU.add)
                k.act(rs[:, ti:ti + 1], rs[:, ti:ti + 1], AF.Sqrt)
                k.recip(rs[:, ti:ti + 1], rs[:, ti:ti + 1])
                k.stt("dve", tm, yt, rs[:, ti:ti + 1], CG[ci], ALU.mult, ALU.mult)
                k.tt("pool", tm, tm, xt, ALU.add)
                k.dma("sp", dst[ti * 128:(ti + 1) * 128, :], tm)
                n += 1
        A.release()

    XRsrc = [x_in]
    marks = []

    def mark(name):
        marks.append((name, P.cnt['dve'], P.cnt['act'], P.cnt['pe']))

    SG = [dscr(f"SG{d}", [AW, NTP]) for d in range(2)]
    AS = [dscr(f"AS{d}", [AW, NTP]) for d in range(2)]
    SCR = {nm: [dscr(f"SC{nm}{d}", [AW, NTP], BF16) for d in range(2)] for nm in ("r", "b", "k", "a")}
    TMB = [dscr(f"TMB{d}", [NT, AW], BF16) for d in range(2)]
    TMKP = [dscr(f"TMKP{d}", [NT, AW], BF16) for d in range(2)]
    TMAP = [dscr(f"TMAP{d}", [NT, AW], BF16) for d in range(2)]
    TMV = dscr("TMV", [NT, AW], BF16)
    PLS = [dscr(f"PLS{d}", [AW, cfg.NCH]) for d in range(2)]
    OD = [dscr(f"OD{d}", [AW, NTP]) for d in range(2)]
    BON = dscr("BON", [AW, NTP])
    NJ = AW // 128

    def cblocks(c0, c1):
        return [(a, min(512, c1 - a)) for a in range(c0, c1, 512)]

    cranges = [(0, cfg.TS + 2, [0]), (cfg.TS + 1, NTP, list(range(1, len(cfg.seqs))))]

    def pcol(si):
        return cfg.seqs[si][0] + si + 1

    def phase_lora(i):
        A.mark()
        wl = alloc([NTP])
        tl = alloc([NTP], BF16)
        wup = alloc([AW], BF16)
        b0 = alloc([2, 2, NJ])
        rows = [alloc([NTP]) for _ in range(2)]
        k.dma("sp", b0[:, 0], e_w0[i])
        k.dma("sp", b0[:, 1], e_a0[i])
        n = 0
        for kind in range(2):
            for d in range(2):
                r0 = 3 * AW + kind * 128 + d * 64
                k.dma("sp", wl[0:64], PT[r0:r0 + 64, :])
                k.act(tl[0:64], wl[0:64], AF.Tanh if kind == 0 else AF.Copy)
                k.dma("pool", wup[0:64], (e_w_up if kind == 0 else e_a_up)[i, d])
                for j in range(NJ):
                    row = rows[n % 2]
                    n += 1
                    for bi, (c0, sz) in enumerate(cblocks(0, NTP)):
                        pb = ps[bi % 8]
                        k.mm(pb[:, 0:sz], [(wup[0:64, j * 128:(j + 1) * 128], tl[0:64, c0:c0 + sz])])
                        k.act(row[:, c0:c0 + sz], pb[:, 0:sz], AF.Sigmoid, bias=b0[:, kind, d, j:j + 1])
                    k.dma("sp", (SG if kind == 0 else AS)[d][j * 128:(j + 1) * 128, :], row)
        A.release()

    xranges = []
    for si_, (off_, Tn_, ci_) in enumerate(cfg.seqs):
        if ci_ == 1:
            n_ = Tn_ // 128
            for n0_ in range(0, n_, 8):
                n1_ = min(n_, n0_ + 8)
                xranges.append((pcol(si_) + n0_ * 128, pcol(si_) + n1_ * 128, [(si_, n0_, n1_)]))
    if len(cfg.seqs) > 1:
        xranges.append((pcol(1), NTP - 1, [(si_, 0, cfg.seqs[si_][1] // 128) for si_ in range(1, len(cfg.seqs))]))
    XW = max(c1_ - c0_ for c0_, c1_, _ in xranges)

    def run_window(jobfns, nslots):
        pending = list(jobfns)
        active = {}
        while pending or active:
            for sl in range(nslots):
                if sl not in active and pending:
                    active[sl] = pending.pop(0)(sl)
            for sl in list(active):
                try:
                    next(active[sl])
                except StopIteration:
                    del active[sl]

    def phase_prep(i):
        A.mark()
        vec = alloc([5, NJ])
        k.dma("sp", vec, e_vec[i])
        NSLOT = 2
        slots = []
        for _ in range(NSLOT):
            sl = dict(rm=alloc([2, XW]), r_=alloc([XW]), kx=alloc([XW]), kk=alloc([XW]), kbar=alloc([XW]),
                      R=[alloc([XW]) for _ in range(7)],
                      ob={nm: alloc([XW], BF16) for nm in ("r", "b", "k", "a", "kp", "ap")},
                      vb=alloc([XW], BF16), sq=alloc([XW], BF16),
                      tms=[alloc([4, 128], BF16) for _ in range(2)], plt=alloc([cfg.NCH]), tmn=[0], rmr=[None])
            slots.append(sl)

        def job(rng, j):
            def gen(sli):
                S = slots[sli]
                c0, c1, segs = rng
                W = c1 - c0
                rm = S["rm"][:, :, 0:W]
                r_, kx, kk, kbar = (S[n_][:, 0:W] for n_ in ("r_", "kx", "kk", "kbar"))
                R = [t_[:, 0:W] for t_ in S["R"]]
                ob = {n_: t_[:, 0:W] for n_, t_ in S["ob"].items()}
                vb = S["vb"][:, 0:W]; sq = S["sq"][:, 0:W]
                tms, plt, tmn = S["tms"], S["plt"], S["tmn"]
                if S["rmr"][0] != (c0, c1):
                    k.dma("sp", rm, cin["rmask"][:, :, c0:c1])
                    S["rmr"][0] = (c0, c1)

                def transposes(src, dst):
                    for (si, n0, n1) in segs:
                        off, Tn, _ = cfg.seqs[si]
                        pc = pcol(si) - c0
                        for cb in range(n0, n1, 4):
                            nb = min(4, n1 - cb)
                            pb = nbank()
                            st = tms[tmn[0] % 2]
                            tmn[0] += 1
                            k.mms([(pb[:, q * 128:(q + 1) * 128], [(src[:, pc + (cb + q) * 128: pc + (cb + q + 1) * 128], identb)])
                                   for q in range(nb)])
                            evac(st[:, 0:nb, :], pb[:, 0:nb * 128].re("p (q c) -> p q c", q=nb))
                            t0 = off + cb * 128
                            k.dma("act", dst[t0:t0 + nb * 128, j * 128:(j + 1) * 128].re("(q p) c -> p q c", p=128), st[:, 0:nb, :])
                            yield

                k.dma("sp", r_, PT[j * 128:(j + 1) * 128, c0:c1])
                k.dma("sp", kx, PT[AW + j * 128:AW + (j + 1) * 128, c0:c1])
                k.dma("sp", R[0], PT[2 * AW + j * 128:2 * AW + (j + 1) * 128, c0:c1])
                k.copy("act", vb, R[0])
                yield from transposes(vb, TMV)
                k.act(kk, kx, AF.Copy, scale=vec[:, 0, j:j + 1])
                k.act(sq, kk, AF.Square)
                yield
                for bi, (a0, sz) in enumerate(cblocks(0, W)):
                    pb = nbank()
                    k.mm(pb[:, 0:sz], [(bones, sq[:, a0:a0 + sz])])
                    k.ts("dve", R[1][:, a0:a0 + sz], pb[:, 0:sz], 1e-24, ALU.max)
                yield
                k.act(R[1], R[1], AF.Ln)
                k.act(R[1], R[1], AF.Exp, scale=-0.5)
                yield
                k.tt("dve", kk, kk, R[1], ALU.mult)
                yield
                for d in range(2):
                    sg, a, cs, cm, em, el, a2 = R
                    k.dma("sp", sg, SG[d][j * 128:(j + 1) * 128, c0:c1])
                    k.dma("sp", a, AS[d][j * 128:(j + 1) * 128, c0:c1])
                    k.tt("dve", a2, kk, a, ALU.mult)
                    k.ts("dve", a, a, -1.0, ALU.add, vec[:, 1, j:j + 1], ALU.mult)
                    yield
                    k.stt("dve", a, a, 1.0, kx, ALU.add, ALU.mult)
                    kd = a
                    if d == 0:
                        k.copy("act", kbar, kd)
                    else:
                        k.tt("dve", kbar, kbar, kd, ALU.add)
                    yield
                    if d == 0:
                        k.scan(cs, rm[:, 0, :], sg, 0.0, ALU.mult, ALU.add)
                    else:
                        k.scan(cs[:, ::-1], rm[:, 1, ::-1], sg[:, ::-1], 0.0, ALU.mult, ALU.add)
                    yield
                    k.tt("dve", cm, cs, sg, ALU.subtract)
                    ep = sg
                    k.act(ep, cs, AF.Exp, scale=-LAM)
                    k.act(em, cs, AF.Exp, scale=LAM)
                    yield
                    k.act(cm, cm, AF.Exp, scale=-LAM)
                    for (si, n0, n1) in segs:
                        off, Tn, _ = cfg.seqs[si]
                        pc = pcol(si) - c0 + n0 * 128
                        nch = n1 - n0
                        Wd = nch * 128
                        e3 = ep[:, pc:pc + Wd].re("p (n l) -> p n l", l=128)
                        edge = e3[:, :, 127:128] if d == 0 else e3[:, :, 0:1]
                        k.tt("dve", el[:, pc:pc + Wd].re("p (n l) -> p n l", l=128),
                             em[:, pc:pc + Wd].re("p (n l) -> p n l", l=128), edge.bcast([128, nch, 128]), ALU.mult)
                        ch = off // 128 + n0
                        k.copy("pool", plt[:, ch:ch + nch], edge.re("p n o -> p (n o)"))
                        k.dma("act", PLS[d][j * 128:(j + 1) * 128, ch:ch + nch], plt[:, ch:ch + nch])
                    yield
                    k.tt("dve", ob["r"], r_, ep, ALU.mult)
                    k.tt("dve", ob["b"], kk, cm, ALU.mult)
                    yield
                    k.tt("dve", ob["k"], kd, em, ALU.mult)
                    k.stt("dve", ob["a"], a2, -1.0, em, ALU.mult, ALU.mult)
                    yield
                    k.tt("dve", ob["kp"], kd, el, ALU.mult)
                    k.stt("dve", ob["ap"], a2, -1.0, el, ALU.mult, ALU.mult)
                    yield
                    for nm in ("r", "b", "k", "a"):
                        k.dma("sp", SCR[nm][d][j * 128:(j + 1) * 128, c0:c1], ob[nm])
                    yield from transposes(ob["b"], TMB[d])
                    yield from transposes(ob["kp"], TMKP[d])
                    yield from transposes(ob["ap"], TMAP[d])
                k.tt("dve", kbar, kbar, r_, ALU.mult)
                k.ts("dve", sq, kbar, vec[:, 2, j:j + 1], ALU.mult, 0.5, ALU.mult)
                k.dma("sp", R[0], PT[2 * AW + j * 128:2 * AW + (j + 1) * 128, c0:c1])
                yield
                for bi, (a0, sz) in enumerate(cblocks(0, W)):
                    pb = nbank()
                    k.mm(pb[:, 0:sz], [(bones, sq[:, a0:a0 + sz])])
                    k.tt("dve", R[1][:, a0:a0 + sz], pb[:, 0:sz], R[0][:, a0:a0 + sz], ALU.mult)
                k.dma("sp", BON[j * 128:(j + 1) * 128, c0:c1], R[1])
                yield
            return gen

        run_window([job(rng, j) for rng in xranges for j in range(NJ)], NSLOT)
        A.release()

    bank_rr = [0]

    def nbank():
        bank_rr[0] = (bank_rr[0] + 1) % 8
        return ps[bank_rr[0]]

    def run_jobs(jobs):
        jobs = list(jobs)
        while jobs:
            nxt = []
            for g in jobs:
                try:
                    next(g)
                    nxt.append(g)
                except StopIteration:
                    pass
            jobs = nxt

    def phase_scan(i):
        A.mark()
        mk = {nm: alloc([4, 128]) for nm in ("m_lt", "m_gt", "m_ge", "m_le", "m_lt_bd", "m_gt_bd", "m_ll", "m_ur")}
        for nm in mk:
            k.dma("sp", mk[nm], cin[nm])

        def blk(pb, nb, w):
            return pb[:, 0:nb * w].re("p (q l) -> p q l", q=nb)

        def blk64(pb, nb, w):
            return pb[0:64, 0:nb * w].re("p (q l) -> p q l", q=nb)

        CH_KB = 32
        chain_base = []
        for _ in range(2):
            chain_base.append(A.p)
            A.p += CH_KB * 1024
        chain_ptr = [0, 0]

        def calloc(slot, shape, dt=F32):
            save = A.p
            A.p = chain_ptr[slot]
            t_ = alloc(shape, dt)
            assert A.p <= chain_base[slot] + CH_KB * 1024, "chain slot overflow"
            chain_ptr[slot] = A.p
            A.p = save
            return t_

        def unit_setup(si, d, hl_, slot):
            off, Tn, ci = cfg.seqs[si]
            n = Tn // 128
            G = len(hl_)
            NV = G * n
            pc = pcol(si)
            ch0 = off // 128
            u = dict(si=si, d=d, hl=hl_, n=NV, n1=n, G=G, pc=pc, Tn=Tn, ci=ci, off=off)
            if d == 0:
                u["masks"] = (mk["m_lt_bd"], mk["m_gt_bd"], mk["m_gt"], mk["m_ge"], mk["m_ll"])
            else:
                u["masks"] = (mk["m_gt_bd"], mk["m_lt_bd"], mk["m_lt"], mk["m_le"], mk["m_ur"])
            for nm in ("r", "b", "k", "a"):
                t_ = alloc([NV * 128], BF16)
                for gi_, h in enumerate(hl_):
                    k.dma("sp", t_[0:64, gi_ * Tn:(gi_ + 1) * Tn], SCR[nm][d][h * 64:(h + 1) * 64, pc:pc + Tn])
                u[nm + "T"] = t_
            BY = alloc([NV, 128], BF16)
            KP = alloc([NV, 64], BF16); APn = alloc([NV, 64], BF16); V = alloc([NV, 64], BF16)
            pl = alloc([NV])
            for gi_, h in enumerate(hl_):
                hs = slice(h * 64, (h + 1) * 64)
                vs_ = slice(gi_ * n, (gi_ + 1) * n)
                k.dma("act", BY[:, vs_, 0:64], TMB[d][off:off + Tn, hs].re("(n t) c -> t n c", t=128))
                k.dma("act", KP[:, vs_, :], TMKP[d][off:off + Tn, hs].re("(n t) c -> t n c", t=128))
                k.dma("act", APn[:, vs_, :], TMAP[d][off:off + Tn, hs].re("(n t) c -> t n c", t=128))
                k.dma("act", V[:, vs_, :], TMV[off:off + Tn, hs].re("(n t) c -> t n c", t=128))
                k.dma("sp", pl[0:64, vs_], PLS[d][hs, ch0:ch0 + n])
            DPt = alloc([NV, 64])
            k.tt("pool", DPt[0:64], identf[0:64, None, 0:64].bcast([64, NV, 64]),
                 pl[0:64, :, None].bcast([64, NV, 64]), ALU.mult)
            SB = calloc(slot, [G, n + 1, 64], BF16)
            S32 = calloc(slot, [G, 64])
            first = 0 if d == 0 else n
            if ci == 1:
                s0 = alloc([G, 64])
                pb = nbank()
                for gi_, h in enumerate(hl_):
                    k.dma("sp", s0[0:64, gi_, :], state_in[i, d, h])
                k.mms([(pb[0:64, gi_ * 64:(gi_ + 1) * 64], [(s0[0:64, gi_, :], identf[0:64, 0:64])]) for gi_ in range(G)])
                k.copy("dve", SB[0:64, :, first, :], pb[0:64, 0:G * 64].re("p (g v) -> p g v", g=G))
            else:
                k.memset("pool", SB[0:64, :, first, :], 0.0)
            OTp = calloc(slot, [G, Tn + 2])
            k.memset("pool", OTp[0:64, :, 0:1], 0.0)
            k.memset("pool", OTp[0:64, :, Tn + 1:Tn + 2], 0.0)
            u.update(BY=BY, KP=KP, APn=APn, V=V, DPt=DPt, SB=SB, S32=S32, OTp=OTp,
                     O0T=calloc(slot, [NV * 128]), RmT=calloc(slot, [NV * 128], BF16), GT=calloc(slot, [NV, 64], BF16),
                     Hh=calloc(slot, [NV, 64]), so=calloc(slot, [G, 64]))
            return u

        def batch_job(u, cb):
            n = u["n"]
            nb = min(4, n - cb)
            cs = list(range(cb, cb + nb))
            mX, mXT, mBK, mRK, mXo = u["masks"]
            rT, bT, kT, aT = u["rT"], u["bT"], u["kT"], u["aT"]
            BY, KP, APn, V = u["BY"], u["KP"], u["APn"], u["V"]
            cl = slice(cb, cb + nb)
            lc = slice(0, nb)
            X = alloc([nb, 128], BF16); XT = alloc([nb, 128], BF16); Xo = alloc([nb, 128], BF16)
            Td = alloc([nb, 128], BF16); P1 = alloc([nb, 128], BF16)
            BKT = alloc([nb, 128], BF16); RKT = alloc([nb, 128], BF16); RAT = alloc([nb, 128], BF16)
            TTb = alloc([nb, 128], BF16)
            Mb = [alloc([nb, 128], BF16) for _ in range(2)]
            MTb = [alloc([nb, 128], BF16) for _ in range(2)]
            WU = alloc([nb, 128], BF16)

            def prod(pb, lt, rt_):
                k.mms([(pb[:, q * 128:(q + 1) * 128],
                        [(lt[0:64, c * 128:(c + 1) * 128], rt_[0:64, c * 128:(c + 1) * 128])])
                       for q, c in enumerate(cs)])

            def sq(pb, lts, rts):
                k.mms([(pb[:, q * 128:(q + 1) * 128], [(lts[:, q, :], rts[:, q, :])]) for q in range(nb)])

            pb = nbank(); prod(pb, bT, aT)
            k.tt("dve", X, blk(pb, nb, 128), mX[:, 0:nb, :], ALU.mult)
            k.tt("dve", Xo, blk(pb, nb, 128), mXo[:, 0:nb, :], ALU.mult)
            yield
            pb = nbank(); prod(pb, aT, bT)
            k.tt("dve", XT, blk(pb, nb, 128), mXT[:, 0:nb, :], ALU.mult)
            k.tt("pool", TTb, XT, identf[:, None, :].bcast([128, nb, 128]), ALU.add)
            yield
            pb = nbank(); prod(pb, kT, bT)
            k.tt("dve", BKT, blk(pb, nb, 128), mBK[:, 0:nb, :], ALU.mult)
            yield
            pb = nbank(); prod(pb, kT, rT)
            k.tt("dve", RKT, blk(pb, nb, 128), mRK[:, 0:nb, :], ALU.mult)
            yield
            pb = nbank(); prod(pb, aT, rT)
            k.tt("dve", RAT, blk(pb, nb, 128), mRK[:, 0:nb, :], ALU.mult)
            yield
            M, MT = X, XT
            for lv in range(5):
                M2, MT2 = Mb[lv % 2], MTb[lv % 2]
                pb = nbank(); sq(pb, MT, M)
                k.copy("act", M2, blk(pb, nb, 128))
                if lv < 4:
                    pb = nbank(); sq(pb, M, MT)
                    k.copy("dve", MT2, blk(pb, nb, 128))
                yield
                pb = nbank(); sq(pb, M2, TTb)
                k.tt("dve", TTb, TTb, blk(pb, nb, 128), ALU.add)
                M, MT = M2, MT2
                yield
            pb = nbank()
            k.mms([(pb[:, q * 128:(q + 1) * 128], [(TTb[:, q, :], identb)]) for q in range(nb)])
            k.copy("act", Td, blk(pb, nb, 128))
            pb = nbank(); sq(pb, Xo, TTb)
            k.copy("dve", P1, blk(pb, nb, 128))
            yield
            pb = nbank(); sq(pb, Td, P1)
            k.tt("dve", TTb, TTb, blk(pb, nb, 128), ALU.add)
            yield
            pb = nbank()
            k.mms([(pb[:, q * 64:(q + 1) * 64], [(BKT[:, q, :], V[:, c, :])]) for q, c in enumerate(cs)])
            k.copy("act", BY[:, cl, 64:128], blk(pb, nb, 64))
            yield
            pb = nbank()
            k.mms([(pb[:, q * 128:(q + 1) * 128], [(TTb[:, q, :], BY[:, c, :])]) for q, c in enumerate(cs)])
            k.copy("act", WU, blk(pb, nb, 128))
            yield
            pb = nbank()
            k.mms([(pb[0:64, q * 128:(q + 1) * 128],
                    [(V[:, c, :], RKT[:, q, :]), (WU[:, q, 64:128], RAT[:, q, :])]) for q, c in enumerate(cs)])
            k.copy("dve", u["O0T"][0:64, cb * 128:(cb + nb) * 128], pb[0:64, 0:nb * 128])
            pb = nbank()
            k.mms([(pb[0:64, q * 128:(q + 1) * 128],
                    [(WU[:, q, 0:64], RAT[:, q, :]), (identb[0:64, 0:64], rT[0:64, c * 128:(c + 1) * 128])])
                   for q, c in enumerate(cs)])
            k.copy("act", u["RmT"][0:64, cb * 128:(cb + nb) * 128], pb[0:64, 0:nb * 128])
            yield
            pb = nbank()
            k.mms([(pb[0:64, q * 64:(q + 1) * 64], [(WU[:, q, 0:64], APn[:, c, :])]) for q, c in enumerate(cs)])
            k.tt("dve", u["GT"][0:64, cl, :], blk64(pb, nb, 64), u["DPt"][0:64, cl, :], ALU.add)
            pb = nbank()
            k.mms([(pb[0:64, q * 64:(q + 1) * 64],
                    [(KP[:, c, :], V[:, c, :]), (APn[:, c, :], WU[:, q, 64:128])]) for q, c in enumerate(cs)])
            k.copy("act", u["Hh"][0:64, cl, :], blk64(pb, nb, 64))
            yield

        def chain_job(u):
            n, d, ci, G = u["n1"], u["d"], u["ci"], u["G"]
            SB, S32 = u["SB"], u["S32"]
            GT4 = u["GT"].re("p (g c) v -> p g c v", g=G)
            H4 = u["Hh"].re("p (g c) v -> p g c v", g=G)
            order = list(range(n)) if d == 0 else list(range(n - 1, -1, -1))
            for ii, c in enumerate(order):
                before = c if d == 0 else c + 1
                after = c + 1 if d == 0 else c
                pb = nbank()
                k.mms([(pb[0:64, g * 64:(g + 1) * 64], [(GT4[0:64, g, c, :], SB[0:64, g, before, :])]) for g in range(G)])
                pv = pb[0:64, 0:G * 64].re("p (g v) -> p g v", g=G)
                k.tt("dve", SB[0:64, :, after, :], pv, H4[0:64, :, c, :], ALU.add)
                if ii == n - 1 and ci == 0:
                    k.tt("dve", S32[0:64], pv, H4[0:64, :, c, :], ALU.add)
                yield
            if ci == 0:
                pb = nbank()
                k.mms([(pb[0:64, g * 64:(g + 1) * 64], [(S32[0:64, g, :], identf[0:64, 0:64])]) for g in range(G)])
                so = u["so"]
                k.copy("act", so[0:64], pb[0:64, 0:G * 64].re("p (g v) -> p g v", g=G))
                h0_ = u["hl"][0]
                k.dma("sp", st_out[u["si"] - 1, i, d, h0_:h0_ + G].re("h v k -> v h k"), so[0:64])
                yield

        def out_job(u):
            n, d, G, NV, Tn = u["n1"], u["d"], u["G"], u["n"], u["Tn"]
            SB, RmT, O0T, OTp = u["SB"], u["RmT"], u["O0T"], u["OTp"]
            for cb in range(0, NV, 4):
                nb = min(4, NV - cb)
                cs = list(range(cb, cb + nb))
                pb = nbank()
                k.mms([(pb[0:64, q * 128:(q + 1) * 128],
                        [(SB[0:64, cv // n, ((cv % n) if d == 0 else (cv % n) + 1), :], RmT[0:64, cv * 128:(cv + 1) * 128])])
                       for q, cv in enumerate(cs)])
                g0_ = cb // n
                if n >= 4:
                    c0_ = (cb % n) * 128
                    k.tt("dve", OTp[0:64, g0_, 1 + c0_:1 + c0_ + nb * 128], pb[0:64, 0:nb * 128],
                         O0T[0:64, cb * 128:(cb + nb) * 128], ALU.add)
                else:
                    ng = nb // n
                    k.tt("dve", OTp[0:64, g0_:g0_ + ng, 1:1 + Tn], pb[0:64, 0:nb * 128].re("p (g t) -> p g t", g=ng),
                         O0T[0:64, cb * 128:(cb + nb) * 128].re("p (g t) -> p g t", g=ng), ALU.add)
                yield
            for gi_, h in enumerate(u["hl"]):
                k.dma("sp", OD[d][h * 64:(h + 1) * 64, u["pc"] - 1:u["pc"] + Tn + 1], OTp[0:64, gi_, :])

        def tail_job(us):
            for u in us:
                yield from chain_job(u)
            gens = [out_job(u) for u in us]
            while gens:
                nxt = []
                for g_ in gens:
                    try:
                        next(g_)
                        nxt.append(g_)
                    except StopIteration:
                        pass
                gens = nxt
                yield

        groups = []
        for si, (off, Tn, ci) in enumerate(cfg.seqs):
            n = Tn // 128
            G = 1 if n > 4 else 4
            for d in range(2):
                for h0 in range(0, cfg.AH, G):
                    groups.append((si, d, list(range(h0, min(cfg.AH, h0 + G)))))
        prev = None
        for gi, (si, d, hs_) in enumerate(groups):
            slot = gi % 2
            chain_ptr[slot] = chain_base[slot]
            A.mark()
            us = [unit_setup(si, d, hs_, slot)]
            jobs = [batch_job(u, cb) for u in us for cb in range(0, u["n"], 4)]
            if prev is not None:
                jobs.append(tail_job(prev))
            run_jobs(jobs)
            A.release()
            prev = us
        run_jobs([tail_job(prev)])
        A.release()

    def phase_gn_job(i):
        vec = alloc([5, NJ])
        k.dma("sp", vec, e_vec[i])
        W0 = XW
        o_ = alloc([W0]); o2_ = alloc([W0]); ob16_ = alloc([W0], BF16); cen_ = alloc([W0]); rstd_ = alloc([W0])
        bon_ = alloc([W0]); ga_ = alloc([W0]); yo_ = alloc([W0], BF16)

        def gen():
            for (c0, c1, segs_) in xranges:
                W = c1 - c0
                o, o2, ob16, cen, rstd, bon, ga, yo = (t_[:, 0:W] for t_ in (o_, o2_, ob16_, cen_, rstd_, bon_, ga_, yo_))
                for j in range(NJ):
                    rs_ = slice(j * 128, (j + 1) * 128)
                    k.dma("sp", o, OD[0][rs_, c0:c1])
                    k.dma("sp", o2, OD[1][rs_, c0:c1])
                    k.dma("act", bon, BON[rs_, c0:c1])
                    k.dma("act", ga, PT[cfg.SHIFT + j * 128: cfg.SHIFT + (j + 1) * 128, c0:c1])
                    k.tt("dve", o, o, o2, ALU.add)
                    k.copy("act", ob16, o)
                    yield
                    for bi, (a0, sz) in enumerate(cblocks(0, W)):
                        pb = nbank()
                        k.mm(pb[:, 0:sz], [(bones, ob16[:, a0:a0 + sz])])
                        k.stt("dve", cen[:, a0:a0 + sz], pb[:, 0:sz], -1.0 / 64, o[:, a0:a0 + sz], ALU.mult, ALU.add)
                    yield
                    k.act(ob16, cen, AF.Square)
                    for bi, (a0, sz) in enumerate(cblocks(0, W)):
                        pb = nbank()
                        k.mm(pb[:, 0:sz], [(bones, ob16[:, a0:a0 + sz])])
                        k.ts("dve", rstd[:, a0:a0 + sz], pb[:, 0:sz], 1.0 / 64, ALU.mult, 64e-5, ALU.add)
                    yield
                    k.act(rstd, rstd, AF.Ln)
                    k.act(rstd, rstd, AF.Exp, scale=-0.5)
                    yield
                    k.tt("dve", cen, cen, rstd, ALU.mult)
                    k.act(cen, cen, AF.Identity, scale=vec[:, 3, j:j + 1], bias=vec[:, 4, j:j + 1])
                    yield
                    k.tt("dve", cen, cen, bon, ALU.add)
                    k.act(ga, ga, AF.Silu)
                    yield
                    k.tt("dve", yo, cen, ga, ALU.mult)
                    k.dma("sp", YT[rs_, c0:c1], yo)
                    yield
        return gen()

    def phase_fft_job(i):
        A.mark()
        BGC, BG = cfg.BGC, cfg.BG
        KC = BGC // 128
        dc = alloc([KC, 2 * BGC], BF16)
        k.dma("sp", dc, cin["dftc"].re("(k p) c -> p k c", p=128))
        u0 = cfg.SHIFT + AW
        g0 = u0 + BW
        for si, (off, Tn, ci) in enumerate(cfg.seqs):
            A.mark()
            pc = pcol(si)
            n = Tn // 128
            ab = alloc([n, BG, 2 * BGC], BF16)
            ub = [alloc([Tn], BF16) for _ in range(2)]
            uf = [alloc([Tn]) for _ in range(2)]
            q = 0
            for g in range(BG):
                us = []
                for kc in range(KC):
                    r0 = u0 + g * BGC + kc * 128
                    f_, b_ = uf[kc % 2], ub[kc % 2]
                    k.dma("sp", f_, PT[r0:r0 + 128, pc:pc + Tn])
                    k.copy("pool", b_, f_)
                    us.append(b_)
                assert KC <= 2
                for t in range(n):
                    for hb in range(0, 2 * BGC, 512):
                        pb = nbank()
                        q += 1
                        k.mm(pb[:, :], [(us[kc][:, t * 128:(t + 1) * 128], dc[:, kc, hb:hb + 512]) for kc in range(KC)])
                        evac(ab[:, t, g, hb:hb + 512], pb[:, :])
                    yield
            TB = min(512, Tn)
            ct = [alloc([n, TB], BF16)] * 2
            stt_ = [alloc([n, TB], BF16)] * 2
            gt = [alloc([TB]) for _ in range(2)]
            yo = [alloc([TB], BF16) for _ in range(2)]
            q = 0
            for tb in range(0, Tn, TB):
                c_, s_ = ct[(tb // TB) % 2], stt_[(tb // TB) % 2]
                k.dma("act", c_, cin[f"dct{Tn}"][:, tb:tb + TB].re("(n p) t -> p n t", p=128))
                k.dma("act", s_, cin[f"dst{Tn}"][:, tb:tb + TB].re("(n p) t -> p n t", p=128))
                for g in range(BG):
                    for cc in range(BGC // 128):
                        pb = nbank()
                        g_ = gt[q % 2]
                        y_ = yo[q % 2]
                        q += 1
                        row = g * BGC + cc * 128
                        k.dma("sp", g_, PT[g0 + row: g0 + row + 128, pc + tb: pc + tb + TB])
                        k.act(g_, g_, AF.Silu)
                        pairs = []
                        for t in range(n):
                            pairs.append((ab[:, t, g, cc * 128:(cc + 1) * 128], c_[:, t, :]))
                            pairs.append((ab[:, t, g, BGC + cc * 128: BGC + (cc + 1) * 128], s_[:, t, :]))
                        k.mm(pb[:, 0:TB], pairs)
                        k.tt("dve", y_, pb[:, 0:TB], g_, ALU.mult)
                        k.dma("sp", YT[AW + row: AW + row + 128, pc + tb: pc + tb + TB], y_)
                        yield
            A.release()
        A.release()


    KVW = cfg.CKV * 64
    QKR = dscr("QKR", [D + KVW, NTP], BF16)
    VTM = dscr("VTM", [NT, KVW], BF16)

    def phase_attn(i):
        HT = HTbox[0]
        A.mark()
        TS = cfg.TS
        pc0 = pcol(0)
        cosT = alloc([TS]); sinT = alloc([TS])
        k.dma("sp", cosT, cin["ropec"])
        k.dma("sp", sinT, cin["ropes"])
        xr = [alloc([NTP]) for _ in range(2)]
        xs = [alloc([TS]) for _ in range(2)]
        t1 = alloc([TS]); t2 = alloc([TS])
        orow = [alloc([NTP], BF16) for _ in range(2)]
        for c in range((D + KVW) // 128):
            X = xr[c % 2]; Xs = xs[c % 2]; o = orow[c % 2]
            k.dma("sp", X, PT[c * 128:(c + 1) * 128, :])
            for b in range(8):
                k.dma("act", Xs[b * 16:(b + 1) * 16], PT[c * 128 + (b ^ 1) * 16: c * 128 + (b ^ 1) * 16 + 16, pc0:pc0 + TS])
            k.copy("act", o, X)
            k.tt("dve", t1, X[:, pc0:pc0 + TS], cosT, ALU.mult)
            k.tt("pool", t2, Xs, sinT, ALU.mult)
            k.tt("dve", o[:, pc0:pc0 + TS], t1, t2, ALU.add)
            k.dma("sp", QKR[c * 128:(c + 1) * 128, :], o)
        A.release()
        if stop_after == ("attn1", 1):
            raise StopBuild()
        mark('attn_O2')
        A.mark()
        wkv = alloc([KD, 2 * KVW], BF16)
        k.dma("pool", wkv, o_w_in[i][:, D:D + 2 * KVW].re("(k p) c -> p k c", p=128))
        vf = [alloc([KVW]) for _ in range(2)]
        vb = [alloc([KVW], BF16) for _ in range(2)]
        kf = [alloc([KVW]) for _ in range(2)]
        for n_, (ti, ci, si) in enumerate(tiles):
            if n_ >= DBGN:
                break
            off, Tn, _ = cfg.seqs[si]
            tl = slice(ti * 128, (ti + 1) * 128)
            pb = ps[n_ % 4]
            k.mm(pb[:, 0:KVW], [(HT[:, kk, tl], wkv[:, kk, KVW:2 * KVW]) for kk in range(KD)])
            v_ = vf[n_ % 2]; b_ = vb[n_ % 2]
            if DBG == -1:
                continue
            k.copy("act", v_, pb[:, 0:KVW])
            if DBG == -2:
                continue
            k.copy("pool", b_, v_)
            if DBG < 1:
                continue
            k.dma("sp", VTM[tl, :], b_)
            if DBG < 2:
                continue
            if ci == 0:
                t0 = ti * 128 - off
                k.dma("sp", nv_out[si - 1, i, t0:t0 + 128, :], v_)
                if DBG < 3:
                    continue
                pb2 = ps[4 + n_ % 4]
                k.mm(pb2[:, 0:KVW], [(HT[:, kk, tl], wkv[:, kk, 0:KVW]) for kk in range(KD)])
                k_ = kf[n_ % 2]
                k.copy("act", k_, pb2[:, 0:KVW])
                k.dma("sp", nk_out[si - 1, i, t0:t0 + 128, :], k_)
        A.release()
        A.release()
        if stop_after == ("attn2", 1):
            raise StopBuild()
        mark('attn_O3')
        A.mark()
        NPT = cfg.PAST // 128
        ckb = alloc([NPT, KVW], BF16); cvb = alloc([NPT, cfg.CKV, 128], BF16)
        k.dma("pool", ckb, ck_in[i].re("(n p) c -> p n c", p=128))
        k.memset("pool", cvb[:, :, :, 64:128], 1.0)
        for pt_ in range(NPT):
            k.dma("pool", cvb[:, pt_, :, 0:64], cv_in[i][pt_ * 128:(pt_ + 1) * 128, :].re("p (h c) -> p h c", c=64))
        skr = alloc([128], BF16)
        k.memset("pool", skr[0:1, 0:64], 0.0)
        k.memset("pool", skr[0:1, 64:128], 1.0)
        CKT = alloc([cfg.CKV, cfg.PAST], BF16)
        q_ = 0
        for pt in range(NPT):
            for kvh in range(cfg.CKV):
                pb = ps[q_ % 8]; q_ += 1
                k.mm(pb[0:64, 0:128], [(ckb[:, pt, kvh * 64:(kvh + 1) * 64], identb)])
                evac(CKT[0:64, kvh, pt * 128:(pt + 1) * 128], pb[0:64, 0:128])
        snk = alloc([cfg.CH]); esr = alloc([cfg.CH, 128], BF16)
        k.dma("sp", snk[0:1], o_sink[i])
        k.act(snk[0:1], snk[0:1], AF.Exp)
        k.copy("dve", esr[0:1], snk[0:1, :, None].bcast([1, cfg.CH, 128]))
        mp = alloc([4, 128], BF16); mn = alloc([4, 128], BF16)
        k.dma("sp", mp, cin["amaskp"])
        k.dma("sp", mn, cin["amaskn"])
        goff = D + 2 * KVW
        NSLOT = 2
        slots = []
        TM = max(Tn_ for (_, Tn_, _) in cfg.seqs)
        for sl_ in range(NSLOT):
            vt_ = alloc([TM // 128, 128], BF16)
            k.memset("pool", vt_[:, :, 64:128], 1.0)
            slots.append(dict(kT=alloc([TM], BF16), qg=alloc([4, TM], BF16), vt=vt_,
                              gg=alloc([4, TM], BF16), yst=alloc([4, TM], BF16),
                              Pb=[alloc([512], BF16) for _ in range(6)], rc=[alloc([512]) for _ in range(2)],
                              tq=[alloc([512]) for _ in range(2)], pn=[0]))

        def ujob(si, kvh):
            def gen(sli):
                S = slots[sli]
                off, Tn, ci = cfg.seqs[si]
                n = Tn // 128
                pc = pcol(si)
                kT = S["kT"][:, 0:Tn]; qg = S["qg"][:, :, 0:Tn]; vt = S["vt"][:, 0:n, :]
                gg = S["gg"][:, :, 0:Tn]; yst = S["yst"][:, :, 0:Tn]
                Pb, rc, tq, pn = S["Pb"], S["rc"], S["tq"], S["pn"]
                bs = [ps[4 * sli], ps[4 * sli + 1]]
                pOT, pRS = ps[4 * sli + 2], ps[4 * sli + 3]
                k.dma("sp", kT[0:64], QKR[D + kvh * 64: D + (kvh + 1) * 64, pc:pc + Tn])
                for g in range(4):
                    hq = kvh * 4 + g
                    k.dma("sp", qg[0:64, g, :], QKR[hq * 64:(hq + 1) * 64, pc:pc + Tn])
                    k.dma("pool", gg[0:64, g, :], PT[goff + hq * 64: goff + (hq + 1) * 64, pc:pc + Tn])
                k.dma("act", vt[:, :, 0:64], VTM[off:off + Tn, kvh * 64:(kvh + 1) * 64].re("(n t) c -> t n c", t=128))
                k.act(gg[0:64], gg[0:64], AF.Silu)
                yield
                for qb in range(n):
                    rhs = qg[0:64, :, qb * 128:(qb + 1) * 128]
                    if ci == 1:
                        keys = [("s", j) for j in (qb - 1, qb, qb + 1) if 0 <= j < n] + [("c", pt) for pt in range(NPT)]
                    else:
                        keys = [("s", j) for j in range(n)]
                    Ps = []
                    vs = []
                    for jj, (kind, j) in enumerate(keys):
                        pbk = bs[jj % 2]
                        msk = None
                        if kind == "s":
                            pairs = [(kT[0:64, j * 128:(j + 1) * 128], rhs)]
                            if ci == 1 and j == qb - 1:
                                msk = mp
                            if ci == 1 and j == qb + 1:
                                msk = mn
                            vs.append(vt[:, j, 0:64])
                        else:
                            pairs = [(CKT[0:64, kvh, j * 128:(j + 1) * 128], rhs)]
                            vs.append(cvb[:, j, kvh, 0:64])
                        k.mm(pbk[:, :], pairs)
                        P_ = Pb[pn[0] % 6]; pn[0] += 1
                        k.act(P_, pbk[:, :], AF.Exp, scale=0.125)
                        if msk is not None:
                            k.tt("dve", P_, P_, msk.re("p g q -> p (g q)"), ALU.mult)
                        Ps.append(P_)
                        yield
                    k.mm(pOT[0:64, :], [(v_, P_) for v_, P_ in zip(vs, Ps)])
                    k.mm(pRS[0:64, :], [(onesb, P_) for P_ in Ps] + [(onesb[0:1, 0:64], esr[0:1, kvh * 4:(kvh + 1) * 4, :])])
                    r_ = rc[qb % 2]; t_ = tq[qb % 2]
                    k.act(r_[0:64], pRS[0:64, :], AF.Ln)
                    k.act(r_[0:64], r_[0:64], AF.Exp, scale=-1.0)
                    k.tt("dve", t_[0:64], pOT[0:64, :], r_[0:64], ALU.mult)
                    k.tt("pool", yst[0:64, :, qb * 128:(qb + 1) * 128], t_[0:64].re("p (g q) -> p g q", g=4),
                         gg[0:64, :, qb * 128:(qb + 1) * 128], ALU.mult)
                    yield
                for g in range(4):
                    hq = kvh * 4 + g
                    k.dma("sp", YT[hq * 64:(hq + 1) * 64, pc:pc + Tn], yst[0:64, g, :])
                yield
            return gen

        run_window([ujob(si, kvh) for si in range(len(cfg.seqs)) for kvh in range(cfg.CKV)], NSLOT)
        A.release()


    def main():
        for l in range(cfg.DEPTH):
            i = l // 2
            src = x_in if l == 0 else XR
            dst = y_out if l == cfg.DEPTH - 1 else XR
            XRsrc[0] = src
            mark('mod')
            phase_mod(l)
            A.mark()
            HTbox[0] = alloc([KD, NT], BF16)
            mark('norm')
            phase_norm(l, src)
            if l % 2 == 0:
                mark('inproj')
                phase_inproj(e_w_in[i], cfg.EIN, cfg.SHIFT, e_mu[i])
                A.release()
                if stop_after == ("inproj", l):
                    return
                mark('lora')
                phase_lora(i)
                mark('prep')
                phase_prep(i)
                if stop_after == ("prep", l):
                    return
                mark('scan')
                phase_scan(i)
                if stop_after == ("scan", l):
                    return
                mark('gn')
                A.mark()
                gj = phase_gn_job(i)
                run_jobs([gj, phase_fft_job(i)])
                A.release()
                mark('fft')
                if stop_after == ("mix", l):
                    return
                mark('outproj')
                phase_outproj(l, e_w_out[i], YT, dst)
            else:
                mark('inproj')
                phase_inproj(o_w_in[i], cfg.OIN, 0, None)
                mark('attn')
                phase_attn(i)
                mark('outproj')
                phase_outproj(l, o_w_out[i], YT, dst)
            if stop_after == ("layer", l):
                return
        return

    try:
        main()
    except StopBuild:
        pass
    P.barrier()
    P.emit()
    global LAST_STATS
    mark('end')
    LAST_STATS = dict(ops=dict(P.cnt), dmas=dict(P.dn), nsem=P.nsem, marks=marks)
    return nc


def pmajor(v, n=128):
    sh = v.shape
    return np.ascontiguousarray(np.swapaxes(v.reshape(sh[:-1] + (sh[-1] // n, n)), -1, -2))


def make_in_maps(cfg, inp, ncores):
    f = np.float32
    shared = {}
    shared["mod_w"] = np.ascontiguousarray(inp["mod_w"], f)
    shared["mod_b"] = np.ascontiguousarray(inp["mod_b"], f)
    shared["norm_pre"] = np.ascontiguousarray(inp["norm_pre"], f)
    shared["norm_post"] = np.ascontiguousarray(inp["norm_post"], f)
    shared["even_w_in"] = np.ascontiguousarray(inp["even_w_in"], f)
    shared["even_mu"] = pmajor(np.asarray(inp["even_mu"], f))
    shared["even_w0"] = np.ascontiguousarray(np.moveaxis(pmajor(np.asarray(inp["even_w0"], f)), 1, 2))
    shared["even_a0"] = np.ascontiguousarray(np.moveaxis(pmajor(np.asarray(inp["even_a0"], f)), 1, 2))
    shared["even_w_up"] = np.ascontiguousarray(inp["even_w_up"], f)
    shared["even_a_up"] = np.ascontiguousarray(inp["even_a_up"], f)
    vec = np.stack([np.asarray(inp["even_k_k"], f), np.asarray(inp["even_k_a"], f),
                    np.asarray(inp["even_r_k"], f).reshape(cfg.NE, cfg.AW),
                    np.asarray(inp["even_gn_w"], f), np.asarray(inp["even_gn_b"], f)], 1)
    shared["even_vec"] = np.ascontiguousarray(np.moveaxis(pmajor(vec), 1, 2))
    shared["even_w_out"] = np.ascontiguousarray(inp["even_w_out"], f)
    shared["odd_w_in"] = np.ascontiguousarray(inp["odd_w_in"], f)
    shared["odd_sink"] = np.ascontiguousarray(np.asarray(inp["odd_sink"], f)[:, None, :])
    shared["odd_w_out"] = np.ascontiguousarray(inp["odd_w_out"], f)
    for n, v in host_consts(cfg).items():
        shared["c_" + n] = v
    maps = []
    cctx = np.asarray(inp["c_ctx"], f)
    for c in range(ncores):
        m = dict(shared)
        xs = np.asarray(inp["x_sample"][c], f)
        xp = np.asarray(inp["x_prompt"][c * cfg.NP:(c + 1) * cfg.NP], f).reshape(cfg.NP * cfg.TP, cfg.D)
        m["x"] = np.ascontiguousarray(np.concatenate([xs, xp], 0))
        cc = np.stack([cctx, np.asarray(inp["c"][c], f)], -1)
        m["cond"] = np.ascontiguousarray(cc.reshape(cfg.KD, 128, 2).transpose(1, 0, 2))
        m["state"] = np.ascontiguousarray(inp["state_wkv"][c], f)
        m["cache_k"] = np.ascontiguousarray(np.asarray(inp["cache_k"][c], f).reshape(cfg.NO, cfg.PAST, cfg.CKV * 64))
        m["cache_v"] = np.ascontiguousarray(np.asarray(inp["cache_v"][c], f).reshape(cfg.NO, cfg.PAST, cfg.CKV * 64))
        maps.append(m)
    return maps


NCORES = 8
import os as _os
DBG = int(_os.environ.get('KDBG', '9'))
DBGN = int(_os.environ.get('KDBGN', '999'))
LAST_STATS = None


def kernel(**inputs):
    cfg = Cfg()
    inp = {k_: np.asarray(v) for k_, v in inputs.items()}
    nc = build(cfg)
    maps = make_in_maps(cfg, inp, NCORES)
    res = run_bass_kernel_spmd(nc, maps, core_ids=list(range(NCORES)))
    rs = res.results
    B, S, D = NCORES * cfg.NP, cfg.TP, cfg.D
    y_sample = np.stack([np.asarray(r["y"][:cfg.TS], np.float32) for r in rs], 0)
    y_prompt = np.concatenate([np.asarray(r["y"][cfg.TS:], np.float32).reshape(cfg.NP, cfg.TP, D) for r in rs], 0)
    new_state = np.concatenate([np.asarray(r["new_state"], np.float32) for r in rs], 0)
    new_k = np.concatenate([np.asarray(r["new_k"], np.float32) for r in rs], 0).reshape(B, cfg.NO, S, cfg.CKV, 64)
    new_v = np.concatenate([np.asarray(r["new_v"], np.float32) for r in rs], 0).reshape(B, cfg.NO, S, cfg.CKV, 64)
    return (y_prompt, y_sample, new_state, new_k, new_v)
```

```python
import math
from contextlib import ExitStack
import numpy as np
import ml_dtypes
import concourse.bass as bass
import concourse.mybir as mybir
from concourse.bass_utils import run_bass_kernel_spmd

F32 = mybir.dt.float32
BF16 = mybir.dt.bfloat16
AF = mybir.ActivationFunctionType
ALU = mybir.AluOpType
AX = mybir.AxisListType
NPBF = ml_dtypes.bfloat16


class Cfg:
    def __init__(self, D=2048, DEPTH=4, TS=2048, TP=256, NP=4, PAST=256, BG=4, GRID_W=64):
        self.D = D
        self.DEPTH = DEPTH
        self.TS = TS
        self.TP = TP
        self.NP = NP
        self.PAST = PAST
        self.GRID_W = GRID_W
        self.NT = TS + NP * TP
        self.KD = D // 128
        self.NE = (DEPTH + 1) // 2
        self.NO = DEPTH // 2
        self.AW = D // 2
        self.AH = self.AW // 64
        self.LORA = 64
        self.BW = D - self.AW
        self.BG = BG
        self.BGC = self.BW // BG
        self.SHIFT = 3 * self.AW + 4 * self.LORA
        self.EIN = self.SHIFT + self.AW + 2 * self.BW
        self.CH = D // 64
        self.CKV = self.CH // 4
        self.OIN = (self.CH + 2 * self.CKV) * 64 + D
        self.seqs = [(0, TS, 1)] + [(TS + i * TP, TP, 0) for i in range(NP)]
        self.NTP = self.NT + len(self.seqs) + 1
        self.NCH = self.NT // 128


class StopBuild(Exception):
    pass


class Buf:
    __slots__ = ("wb", "rb", "name", "bw", "nb")

    def __init__(self, name="", span=1 << 20, nb=1):
        self.name = name
        self.nb = max(1, nb)
        self.bw = max(1, (span + self.nb - 1) // self.nb)
        self.wb = [[] for _ in range(self.nb)]
        self.rb = [[] for _ in range(self.nb)]

    def buckets(self, box):
        b0 = min(self.nb - 1, box[2] // self.bw)
        b1 = min(self.nb - 1, (box[3] - 1) // self.bw)
        for b in range(b0, b1 + 1):
            lo = max(box[2], b * self.bw) if b < self.nb - 1 or True else box[2]
            hi = min(box[3], (b + 1) * self.bw) if b < self.nb - 1 else box[3]
            yield b, (box[0], box[1], lo, hi)


def _ovl(a, b):
    return a[0] < b[1] and b[0] < a[1] and a[2] < b[3] and b[2] < a[3]


def _contains(a, b):
    return a[0] <= b[0] and a[1] >= b[1] and a[2] <= b[2] and a[3] >= b[3]


def _union(a, b):
    return (min(a[0], b[0]), max(a[1], b[1]), min(a[2], b[2]), max(a[3], b[3]))


def _coarsen(lst):
    d = {}
    for box, s, v in lst:
        if s in d:
            d[s] = (_union(d[s][0], box), max(d[s][1], v))
        else:
            d[s] = (box, v)
    return [(bx, s, v) for s, (bx, v) in d.items()]


MAXLIST = 48


ENGS = ("pe", "act", "dve", "pool", "sp")
EPOCH = 30000
NDMA = 12


class Prog:
    def __init__(self, nc, stack):
        self.nc = nc
        self.stack = stack
        self.ops = {e: [] for e in ENGS}
        self.cnt = {e: 0 for e in ENGS}
        self.esem = {e: [] for e in ENGS}
        self.known = {e: {} for e in ENGS}
        self.semobj = {}
        self.dn = {e: 0 for e in ENGS}
        self.dsem = {e: [] for e in ENGS}
        self.nsem = 0

    def newsem(self, name):
        s = self.stack.enter_context(self.nc.semaphore(name))
        self.nsem += 1
        self.semobj[name] = s
        return name

    def _ticket(self, eng):
        i = self.cnt[eng]
        self.cnt[eng] += 1
        ep = i // EPOCH
        while len(self.esem[eng]) <= ep:
            self.esem[eng].append(self.newsem(f"e_{eng}_{len(self.esem[eng])}"))
        return (self.esem[eng][ep], i % EPOCH + 1)

    def _waits(self, eng, deps):
        kn = self.known[eng]
        out = []
        for s, v in deps.items():
            if kn.get(s, 0) < v:
                kn[s] = v
                out.append((s, v))
        return out

    @staticmethod
    def _merge(d, s, v):
        if d.get(s, 0) < v:
            d[s] = v

    def _deps(self, reads, writes):
        deps = {}
        mg = self._merge
        for b, box in reads:
            for bi, cb in b.buckets(box):
                for bx, s, v in b.wb[bi]:
                    if _ovl(bx, cb):
                        mg(deps, s, v)
        for b, box in writes:
            for bi, cb in b.buckets(box):
                for bx, s, v in b.wb[bi]:
                    if _ovl(bx, cb):
                        mg(deps, s, v)
                for bx, s, v in b.rb[bi]:
                    if _ovl(bx, cb):
                        mg(deps, s, v)
        return deps

    def _commit(self, t, reads, writes):
        ts, tv = t
        for b, box in reads:
            for bi, cb in b.buckets(box):
                lst = b.rb[bi]
                done = False
                for i, (bx, s, v) in enumerate(lst):
                    if s == ts and _contains(bx, cb):
                        if v < tv:
                            lst[i] = (bx, s, tv)
                        done = True
                        break
                if not done:
                    lst = [e for e in lst if not (e[1] == ts and _contains(cb, e[0]))]
                    lst.append((cb, ts, tv))
                    if len(lst) > MAXLIST:
                        lst = _coarsen(lst)
                    b.rb[bi] = lst
        for b, box in writes:
            for bi, cb in b.buckets(box):
                lw = [e for e in b.wb[bi] if not _contains(cb, e[0])]
                lw.append((cb, ts, tv))
                if len(lw) > MAXLIST:
                    lw = _coarsen(lw)
                b.wb[bi] = lw
                b.rb[bi] = [e for e in b.rb[bi] if not _contains(cb, e[0])]

    def op(self, eng, fn, reads=(), writes=()):
        deps = self._deps(reads, writes)
        waits = self._waits(eng, deps)
        t = self._ticket(eng)
        self.ops[eng].append((waits, fn, (t[0], 1)))
        self._commit(t, reads, writes)
        return t

    def dma(self, eng, out, in_, reads=(), writes=(), **kw):
        deps = self._deps(reads, writes)
        n = self.dn[eng]
        self.dn[eng] += 1
        slot = n % NDMA
        while len(self.dsem[eng]) <= slot:
            self.dsem[eng].append(self.newsem(f"d_{eng}_{len(self.dsem[eng])}"))
        s = self.dsem[eng][slot]
        use = n // NDMA
        if use > 0:
            self._merge(deps, s, 16 * use)
        waits = self._waits(eng, deps)
        t = (s, 16 * (use + 1))

        def fn(e, out=out, in_=in_, kw=kw):
            return e.dma_start(out=out, in_=in_, **kw)

        self.ops[eng].append((waits, fn, (s, 16)))
        self._commit(t, reads, writes)
        return t

    def barrier(self):
        deps = {}
        for e in ENGS:
            i = self.cnt[e]
            if i > 0:
                ep = (i - 1) // EPOCH
                self._merge(deps, self.esem[e][ep], (i - 1) % EPOCH + 1)
            n = self.dn[e]
            for slot, s in enumerate(self.dsem[e]):
                uses = (n - slot + NDMA - 1) // NDMA
                if uses > 0:
                    self._merge(deps, s, 16 * uses)
        for e in ENGS:
            waits = self._waits(e, dict(deps))
            if waits:
                self.ops[e].append((waits, None, None))

    def emit(self):
        nc = self.nc
        so = self.semobj
        with nc.Block() as block:
            def run(e, lst):
                for waits, fn, inc in lst:
                    for s, v in waits:
                        e.wait_ge(so[s], v)
                    if fn is not None:
                        ins = fn(e)
                        ins.then_inc(so[inc[0]], inc[1])

            @block.tensor
            def _(e):
                run(e, self.ops["pe"])

            @block.scalar
            def _(e):
                run(e, self.ops["act"])

            @block.vector
            def _(e):
                run(e, self.ops["dve"])

            @block.gpsimd
            def _(e):
                run(e, self.ops["pool"])

            @block.sync
            def _(e):
                run(e, self.ops["sp"])


class Arena:
    def __init__(self, t, nbytes):
        self.t = t
        self.n = nbytes
        self.p = 0
        self.marks = []

    def alloc(self, shape, dtype=F32):
        sz = 4 if dtype == F32 else 2
        cols = int(np.prod(shape))
        nb = (cols * sz + 63) // 64 * 64
        assert self.p + nb <= self.n, f"arena overflow {self.p}+{nb}>{self.n}"
        w0 = self.p // 4
        w1 = (self.p + nb) // 4
        self.p += nb
        ap = self.t[:, w0:w1]
        if dtype != F32:
            ap = ap.bitcast(dtype)
        ap = ap[:, 0:cols]
        if len(shape) == 2:
            ap = ap.rearrange("p (a b) -> p a b", a=shape[0])
        elif len(shape) == 3:
            ap = ap.rearrange("p (a b c) -> p a b c", a=shape[0], b=shape[1])
        return ap

    def mark(self):
        self.marks.append(self.p)

    def release(self):
        self.p = self.marks.pop()


_DT_SZ = {F32: 4, BF16: 2}


class T:
    __slots__ = ("ap", "buf", "dram")

    def __init__(self, ap, buf=None, name="", dram=False):
        self.ap = ap
        if buf is None:
            nbytes = _DT_SZ.get(ap.dtype, 4)
            for d_ in (ap.tensor.shape if dram else ap.tensor.shape[1:]):
                nbytes *= d_
            buf = Buf(name, span=nbytes, nb=(128 if dram else 1))
        self.buf = buf
        self.dram = dram

    def __getitem__(self, key):
        return T(self.ap[key], self.buf, dram=self.dram)

    def re(self, s, **kw):
        return T(self.ap.rearrange(s, **kw), self.buf, dram=self.dram)

    def bc(self, dt):
        return T(self.ap.bitcast(dt), self.buf, dram=self.dram)

    def bcast(self, shape):
        return T(self.ap.broadcast_to(shape), self.buf, dram=self.dram)

    def box(self):
        ap = self.ap
        sz = _DT_SZ.get(ap.dtype, 4)
        dims = ap.ap
        off = ap.offset
        if self.dram:
            lo = hi = off
            for st, cn in dims:
                if st < 0:
                    lo += st * (cn - 1)
                else:
                    hi += st * (cn - 1)
            return (0, 1, lo * sz, (hi + 1) * sz)
        row = 1
        for d_ in ap.tensor.shape[1:]:
            row *= d_
        p0 = off // row
        col = off % row
        pst, pcn = dims[0]
        p1 = p0 + (pcn - 1) * max(1, pst // row) + 1
        lo = hi = col
        for st, cn in dims[1:]:
            if st < 0:
                lo += st * (cn - 1)
            else:
                hi += st * (cn - 1)
        return (p0, p1, lo * sz, (hi + 1) * sz)


def _ap(x):
    return x.ap if isinstance(x, T) else x


def _bufs(*xs):
    return [(x.buf, x.box()) for x in xs if isinstance(x, T)]


class K:
    def __init__(self, P):
        self.P = P

    def mm(self, out, pairs, extra_reads=()):
        aps = [(_ap(l), _ap(r)) for l, r in pairs]
        o = _ap(out)
        n = len(aps)

        def fn(e):
            ins = None
            for i, (l, r) in enumerate(aps):
                ins = e.matmul(o, lhsT=l, rhs=r, start=(i == 0), stop=(i == n - 1))
            return ins

        reads = []
        for l, r in pairs:
            reads += _bufs(l, r)
        return self.P.op("pe", fn, reads=list(reads) + list(extra_reads), writes=_bufs(out))

    def mms(self, groups):
        gl = [(_ap(o), [(_ap(l), _ap(r)) for l, r in pairs]) for o, pairs in groups]

        def fn(e):
            ins = None
            for o, aps in gl:
                n = len(aps)
                for i, (l, r) in enumerate(aps):
                    ins = e.matmul(o, lhsT=l, rhs=r, start=(i == 0), stop=(i == n - 1))
            return ins

        reads, pw = [], []
        for o, pairs in groups:
            pw += _bufs(o)
            for l, r in pairs:
                reads += _bufs(l, r)
        return self.P.op("pe", fn, reads=reads, writes=pw)

    def act(self, out, in_, func, bias=None, scale=None, accum=None, eng="act"):
        kw = {}
        if bias is not None:
            kw["bias"] = _ap(bias)
        if scale is not None:
            kw["scale"] = _ap(scale)
        if accum is not None:
            kw["accum_out"] = _ap(accum)
        o, i = _ap(out), _ap(in_)
        return self.P.op(eng, lambda e: e.activation(out=o, in_=i, func=func, **kw),
                         reads=_bufs(in_, bias, scale), writes=_bufs(out, accum))

    def tt(self, eng, out, a, b, op):
        o, x, y = _ap(out), _ap(a), _ap(b)
        return self.P.op(eng, lambda e: e.tensor_tensor(o, x, y, op), reads=_bufs(a, b), writes=_bufs(out))

    def ts(self, eng, out, a, s1, op0, s2=None, op1=None, accum=None):
        o, x = _ap(out), _ap(a)
        v1, v2 = _ap(s1), _ap(s2)
        kw = {}
        if op1 is not None:
            kw["op1"] = op1
        if accum is not None:
            kw["accum_out"] = _ap(accum)
        return self.P.op(eng, lambda e: e.tensor_scalar(o, x, v1, v2, op0, **kw),
                         reads=_bufs(a, s1, s2), writes=_bufs(out, accum))

    def stt(self, eng, out, a, s, b, op0, op1):
        o, x, y, v = _ap(out), _ap(a), _ap(b), _ap(s)
        return self.P.op(eng, lambda e: e.scalar_tensor_tensor(o, x, v, y, op0, op1),
                         reads=_bufs(a, s, b), writes=_bufs(out))

    def copy(self, eng, out, a):
        o, x = _ap(out), _ap(a)
        if eng == "act":
            return self.P.op(eng, lambda e: e.copy(o, x), reads=_bufs(a), writes=_bufs(out))
        return self.P.op(eng, lambda e: e.tensor_copy(o, x), reads=_bufs(a), writes=_bufs(out))

    def memset(self, eng, out, val):
        o = _ap(out)
        return self.P.op(eng, lambda e: e.memset(o, val), writes=_bufs(out))

    def recip(self, out, a):
        o, x = _ap(out), _ap(a)
        return self.P.op("dve", lambda e: e.reciprocal(o, x), reads=_bufs(a), writes=_bufs(out))

    def scan(self, out, d0, d1, init, op0, op1):
        o, x, y = _ap(out), _ap(d0), _ap(d1)
        return self.P.op("dve", lambda e: e.tensor_tensor_scan(o, x, y, init, op0, op1),
                         reads=_bufs(d0, d1), writes=_bufs(out))

    def dma(self, eng, out, in_, **kw):
        return self.P.dma(eng, _ap(out), _ap(in_), reads=_bufs(in_), writes=_bufs(out), **kw)


LAM = math.exp(-0.5)


def host_consts(cfg):
    c = {}
    c["identb"] = np.eye(128, dtype=np.float32).astype(NPBF)
    c["identf"] = np.eye(128, dtype=np.float32)
    sel = np.zeros((2, 256), np.float32)
    sel[0, 0:128] = 1.0
    sel[1, 128:256] = 1.0
    c["sel"] = sel
    bo = np.zeros((128, 128), np.float32)
    bo[0:64, 0:64] = 1.0
    bo[64:128, 64:128] = 1.0
    c["bones"] = bo.astype(NPBF)
    c["onesb"] = np.ones((128, 64), np.float32).astype(NPBF)
    mf = np.ones((cfg.NTP,), np.float32)
    mb = np.ones((cfg.NTP,), np.float32)
    for si, (off, T, _) in enumerate(cfg.seqs):
        pc = off + si + 1
        for n in range(T // 128):
            mf[pc + n * 128] = 0.0
            mb[pc + n * 128 + 127] = 0.0
    c["rmask"] = np.stack([np.broadcast_to(mf, (128, cfg.NTP)), np.broadcast_to(mb, (128, cfg.NTP))], 1).copy()
    r = np.arange(128)[:, None]
    q = np.arange(128)[None, :]
    def rep(m):
        return np.broadcast_to(m.astype(np.float32)[:, None, :], (128, 4, 128)).copy()
    c["m_lt"] = rep(q < r)
    c["m_gt"] = rep(q > r)
    c["m_ge"] = rep(q >= r)
    c["m_le"] = rep(q <= r)
    bd = ((r < 64) == (q < 64))
    c["m_lt_bd"] = rep((q < r) & bd)
    c["m_gt_bd"] = rep((q > r) & bd)
    c["m_ll"] = rep((r >= 64) & (q < 64))
    c["m_ur"] = rep((r < 64) & (q >= 64))
    def dft(n, scale):
        idx = (np.arange(n)[:, None] * np.arange(n)[None, :]) % n
        ang = 2.0 * np.pi * idx.astype(np.float64) / n
        return (np.cos(ang) * scale), (np.sin(ang) * scale)
    cc, sc = dft(cfg.BGC, cfg.BGC ** -0.5)
    c["dftc"] = np.concatenate([cc, sc], 1).astype(np.float32).astype(NPBF)
    for T in sorted({cfg.TS, cfg.TP}):
        ct, st = dft(T, T ** -0.5)
        c[f"dct{T}"] = ct.astype(np.float32).astype(NPBF)
        c[f"dst{T}"] = (-st).astype(np.float32).astype(NPBF)
    half, nf = 32, 16
    inv = 1.0 / (10000.0 ** (np.arange(nf, dtype=np.float32) / nf))
    pos = np.arange(cfg.TS)
    row = (pos // cfg.GRID_W).astype(np.float32)
    col = (pos % cfg.GRID_W).astype(np.float32)
    cosT = np.zeros((64, cfg.TS), np.float32)
    sinT = np.zeros((64, cfg.TS), np.float32)
    for d in range(64):
        p = row if d < half else col
        dd = d % half
        f = dd % nf
        ang = p * inv[f]
        cosT[d] = np.cos(ang)
        sinT[d] = np.sin(ang) * (-1.0 if dd < nf else 1.0)
    c["ropec"] = np.concatenate([cosT, cosT], 0)
    c["ropes"] = np.concatenate([sinT, sinT], 0)
    jk = np.arange(128)[:, None]
    iq = np.arange(128)[None, :]
    mp = np.where(jk >= iq, 1.0, 0.0)
    mn = np.where(jk <= iq, 1.0, 0.0)
    c["amaskp"] = np.broadcast_to(mp[:, None, :], (128, 4, 128)).astype(np.float32).astype(NPBF).copy()
    c["amaskn"] = np.broadcast_to(mn[:, None, :], (128, 4, 128)).astype(np.float32).astype(NPBF).copy()
    return c


CONST_DT = {"identb": BF16, "bones": BF16, "onesb": BF16, "dftc": BF16, "amaskp": BF16, "amaskn": BF16}


def build(cfg, debug=(), stop_after=None, sbuf_kb=206):
    nc = bass.Bass("TRN2", target_bir_lowering=False)
    stack = ExitStack()
    P = Prog(nc, stack)
    k = K(P)
    D, KD, NT, NTP = cfg.D, cfg.KD, cfg.NT, cfg.NTP
    NE, NO, AW, BW = cfg.NE, cfg.NO, cfg.AW, cfg.BW

    def din(name, shape, dt=F32):
        return T(nc.dram_tensor(name, list(shape), dt, kind="ExternalInput").ap(), name=name, dram=True)

    def dout(name, shape, dt=F32):
        return T(nc.dram_tensor(name, list(shape), dt, kind="ExternalOutput").ap(), name=name, dram=True)

    def dscr(name, shape, dt=F32):
        kind = "ExternalOutput" if name in debug else "Internal"
        return T(nc.dram_tensor(name, list(shape), dt, kind=kind).ap(), name=name, dram=True)

    x_in = din("x", [NT, D])
    cond_in = din("cond", [128, KD, 2])
    state_in = din("state", [NE, 2, cfg.AH, 64, 64])
    ck_in = din("cache_k", [NO, cfg.PAST, cfg.CKV * 64])
    cv_in = din("cache_v", [NO, cfg.PAST, cfg.CKV * 64])
    mod_w = din("mod_w", [cfg.DEPTH, D, 3 * D])
    mod_b = din("mod_b", [cfg.DEPTH, 3 * D])
    npre = din("norm_pre", [cfg.DEPTH, D])
    npost = din("norm_post", [cfg.DEPTH, D])
    e_w_in = din("even_w_in", [NE, D, cfg.EIN])
    e_mu = din("even_mu", [NE, 128, cfg.SHIFT // 128])
    e_w0 = din("even_w0", [NE, 128, 2, AW // 128])
    e_w_up = din("even_w_up", [NE, 2, 64, AW])
    e_a0 = din("even_a0", [NE, 128, 2, AW // 128])
    e_a_up = din("even_a_up", [NE, 2, 64, AW])
    e_vec = din("even_vec", [NE, 128, 5, AW // 128])
    e_w_out = din("even_w_out", [NE, D, D])
    o_w_in = din("odd_w_in", [NO, D, cfg.OIN])
    o_sink = din("odd_sink", [NO, 1, cfg.CH])
    o_w_out = din("odd_w_out", [NO, D, D])
    consts = host_consts(cfg)
    cin = {n: din("c_" + n, v.shape, CONST_DT.get(n, BF16 if v.dtype == NPBF else F32)) for n, v in consts.items()}

    y_out = dout("y", [NT, D])
    st_out = dout("new_state", [cfg.NP, NE, 2, cfg.AH, 64, 64])
    nk_out = dout("new_k", [cfg.NP, NO, cfg.TP, cfg.CKV * 64])
    nv_out = dout("new_v", [cfg.NP, NO, cfg.TP, cfg.CKV * 64])

    XR = dscr("XR", [NT, D])
    PT = dscr("PT", [max(cfg.EIN, cfg.OIN), NTP])
    YT = dscr("YT", [D, NTP], BF16)

    big = stack.enter_context(nc.sbuf_tensor("big", [128, sbuf_kb * 256], F32))
    A = Arena(big, sbuf_kb * 1024)

    bigbuf = Buf("big", span=sbuf_kb * 1024, nb=sbuf_kb * 2)

    def alloc(shape, dt=F32, name=""):
        return T(A.alloc(shape, dt), bigbuf)

    ps = [T(stack.enter_context(nc.psum_tensor(f"ps{i}", [128, 512], F32))[:, :], name=f"ps{i}") for i in range(8)]

    identb = alloc([128], BF16)
    identf = alloc([128])
    sel = alloc([256])
    bones = alloc([128], BF16)
    onesb = alloc([64], BF16)
    k.dma("sp", identb, cin["identb"])
    k.dma("sp", identf, cin["identf"])
    k.dma("sp", sel[0:2], cin["sel"])
    k.dma("sp", bones, cin["bones"])
    k.dma("sp", onesb, cin["onesb"])
    condt = alloc([KD, 2])
    sT = alloc([KD, 2], BF16)
    k.dma("sp", condt, cond_in)
    k.act(sT, condt, AF.Silu)
    gsF = alloc([KD, 2])
    shF = alloc([KD, 2])
    CG = [alloc([D]), alloc([D])]
    HTbox = [None]

    blocks = []
    for si, (off, Tn, ci) in enumerate(cfg.seqs):
        for b0 in range(0, Tn, 512):
            sz = min(512, Tn - b0)
            blocks.append((si, off + b0, sz, off + b0 + si + 1))
    tiles = []
    for si, (off, Tn, ci) in enumerate(cfg.seqs):
        for t0 in range(0, Tn, 128):
            tiles.append(((off + t0) // 128, ci, si))

    ev_rr = [0]

    def evac(out, in_):
        ev_rr[0] ^= 1
        if ev_rr[0]:
            k.act(out, in_, AF.Identity)
        else:
            k.copy("dve", out, in_)

    def phase_mod(l):
        A.mark()
        m = alloc([3 * D])
        mb = alloc([3 * D])
        np2 = alloc([D])
        nq2 = alloc([D])
        gs2 = alloc([D])
        cg2 = alloc([D])
        wt = [alloc([KD, 512], BF16) for _ in range(2)]
        for p in range(2):
            k.dma("sp", mb[p:p + 1], mod_b[l:l + 1, :])
            k.dma("sp", np2[p:p + 1], npre[l:l + 1, :])
            k.dma("sp", nq2[p:p + 1], npost[l:l + 1, :])
        nj = 3 * D // 512
        for j in range(nj):
            w = wt[j % 2]
            k.dma("pool", w, mod_w[l, :, j * 512:(j + 1) * 512].re("(k p) c -> p k c", p=128))
            pb = ps[j % 2]
            k.mm(pb[0:2, :], [(sT[:, kk, :], w[:, kk, :]) for kk in range(KD)])
            k.tt("dve", m[0:2, j * 512:(j + 1) * 512], pb[0:2, :], mb[0:2, j * 512:(j + 1) * 512], ALU.add)
        k.stt("dve", gs2[0:2], m[0:2, D:2 * D], 1.0, np2[0:2], ALU.add, ALU.mult)
        k.tt("dve", cg2[0:2], m[0:2, 2 * D:3 * D], nq2[0:2], ALU.mult)
        i2 = identf[0:2, 0:2]
        k.mms([(ps[2][:, 2 * kk:2 * kk + 2], [(gs2[0:2, kk * 128:(kk + 1) * 128], i2)]) for kk in range(KD)])
        k.mms([(ps[3][:, 2 * kk:2 * kk + 2], [(m[0:2, kk * 128:(kk + 1) * 128], i2)]) for kk in range(KD)])
        k.copy("dve", gsF.re("p k c -> p (k c)"), ps[2][:, 0:2 * KD])
        k.copy("dve", shF.re("p k c -> p (k c)"), ps[3][:, 0:2 * KD])
        for c in range(2):
            for j in range(D // 512):
                pb = ps[4 + (j % 2)]
                k.mm(pb[:, :], [(sel[0:2, c * 128:(c + 1) * 128], cg2[0:2, j * 512:(j + 1) * 512])])
                evac(CG[c][:, j * 512:(j + 1) * 512], pb[:, :])
        A.release()

    def phase_norm(l, src):
        A.mark()
        xb = [alloc([D]) for _ in range(2)]
        xn = [alloc([D], BF16) for _ in range(2)]
        junk = alloc([D], BF16)
        ss = alloc([cfg.NCH])
        rs = alloc([cfg.NCH])
        k.memset("dve", ss, 0.0)
        for n, (ti, ci, si) in enumerate(tiles):
            xt = xb[n % 2]
            k.dma("sp", xt, src[ti * 128:(ti + 1) * 128, :])
            k.act(junk, xt, AF.Square, accum=ss[:, ti:ti + 1])
            k.ts("dve", rs[:, ti:ti + 1], ss[:, ti:ti + 1], 1.0 / D, ALU.mult, 1e-6, ALU.add)
            k.act(rs[:, ti:ti + 1], rs[:, ti:ti + 1], AF.Sqrt)
            k.recip(rs[:, ti:ti + 1], rs[:, ti:ti + 1])
            xq = xn[n % 2]
            k.act(xq, xt, AF.Copy, scale=rs[:, ti:ti + 1])
            for kq in range(KD // 4):
                pb = ps[(n * (KD // 4) + kq) % 8]
                k.mms([(pb[:, j * 128:(j + 1) * 128], [(xq[:, (kq * 4 + j) * 128:(kq * 4 + j + 1) * 128], identb)])
                       for j in range(4)])
                for j in range(4):
                    kk = kq * 4 + j
                    if kq % 2 == 0:
                        k.act(HTbox[0][:, kk, ti * 128:(ti + 1) * 128], pb[:, j * 128:(j + 1) * 128], AF.Identity,
                              scale=gsF[:, kk, ci:ci + 1], bias=shF[:, kk, ci:ci + 1])
                    else:
                        k.ts("dve", HTbox[0][:, kk, ti * 128:(ti + 1) * 128], pb[:, j * 128:(j + 1) * 128],
                             gsF[:, kk, ci:ci + 1], ALU.mult, shF[:, kk, ci:ci + 1], ALU.add)
        A.release()

    def phase_inproj(W, ncol, nshift, mu):
        A.mark()
        NC4 = 2
        wt = [alloc([KD, 128 * NC4], BF16) for _ in range(2)]
        rows = [alloc([NTP]) for _ in range(2)]
        rb = alloc([NTP])
        ro = [alloc([NTP]) for _ in range(2)]
        for r_ in rows + [rb] + ro:
            k.memset("pool", r_, 0.0)
        if nshift:
            mut = alloc([nshift // 128])
            amu = alloc([nshift // 128])
            bmu = alloc([nshift // 128])
            k.dma("sp", mut, mu)
            k.ts("dve", amu, mut, -1.0, ALU.mult, 1.0, ALU.add)
            k.ts("dve", bmu, mut, 0.5, ALU.mult)
        nchunk = ncol // 128
        pbi = 0
        for c in range(nchunk):
            if c % NC4 == 0:
                w = wt[(c // NC4) % 2]
                wc = min(128 * NC4, ncol - c * 128)
                k.dma("pool", w[:, :, 0:wc], W[:, c * 128:c * 128 + wc].re("(k p) c -> p k c", p=128))
            cc = c % NC4
            row = rows[c % 2]
            for (si, t0, sz, pc) in blocks:
                pb = ps[pbi % 8]
                pbi += 1
                k.mm(pb[:, 0:sz], [(w[:, kk, cc * 128:(cc + 1) * 128], HTbox[0][:, kk, t0:t0 + sz]) for kk in range(KD)])
                evac(row[:, pc:pc + sz], pb[:, 0:sz])
            if c < nshift // 128:
                o = ro[c % 2]
                k.act(o[:, 1:NTP - 1], row[:, 1:NTP - 1], AF.Copy, scale=amu[:, c:c + 1])
                k.tt("pool", rb[:, 1:NTP - 1], row[:, 0:NTP - 2], row[:, 2:NTP], ALU.add)
                k.stt("dve", o[:, 1:NTP - 1], rb[:, 1:NTP - 1], bmu[:, c:c + 1], o[:, 1:NTP - 1], ALU.mult, ALU.add)
                k.dma("sp", PT[c * 128:(c + 1) * 128, :], o)
            else:
                k.dma("sp", PT[c * 128:(c + 1) * 128, :], row)
        A.release()

    def phase_outproj(l, Wout, src, dst):
        A.mark()
        wo = alloc([KD, D], BF16)
        for q in range(D // 512):
            k.dma("pool", wo[:, :, q * 512:(q + 1) * 512], Wout[:, q * 512:(q + 1) * 512].re("(k p) c -> p k c", p=128))
        yb = [alloc([KD, 512], BF16) for _ in range(2)]
        ysb = [alloc([D]) for _ in range(2)]
        xb = [alloc([D]) for _ in range(2)]
        tmp = [alloc([D]) for _ in range(2)]
        junk = alloc([D], BF16)
        ss = alloc([cfg.NCH])
        rs = alloc([cfg.NCH])
        k.memset("dve", ss, 0.0)
        nb = D // 512
        n = 0
        for bi, (si, t0, sz, pc) in enumerate(blocks):
            ci = cfg.seqs[si][2]
            ybt = yb[bi % 2]
            k.dma("sp", ybt[:, :, 0:sz], src[:, pc:pc + sz].re("(k p) t -> p k t", p=128))
            for j in range(sz // 128):
                ti = (t0 + j * 128) // 128
                yt = ysb[n % 2]
                xt = xb[n % 2]
                tm = tmp[n % 2]
                k.dma("act", xt, XRsrc[0][ti * 128:(ti + 1) * 128, :])
                for q in range(nb):
                    pb = ps[(n % 2) * 4 + (q % 4)]
                    k.mm(pb[:, :], [(ybt[:, kk, j * 128:(j + 1) * 128], wo[:, kk, q * 512:(q + 1) * 512]) for kk in range(KD)])
                    evac(yt[:, q * 512:(q + 1) * 512], pb[:, :])
                k.act(junk, yt, AF.Square, accum=ss[:, ti:ti + 1])
                k.ts("dve", rs[:, ti:ti + 1], ss[:, ti:ti + 1], 1.0 / D, ALU.mult, 1e-6, ALU.add)
                k.act(rs[:, ti:ti + 1], rs[:, ti:ti + 1], AF.Sqrt)
                k.recip(rs[:, ti:ti + 1], rs[:, ti:ti + 1])
                k.stt("dve", tm, yt, rs[:, ti:ti + 1], CG[ci], ALU.mult, ALU.mult)
                k.tt("pool", tm, tm, xt, ALU.add)
                k.dma("sp", dst[ti * 128:(ti + 1) * 128, :], tm)
                n += 1
        A.release()

    XRsrc = [x_in]
    marks = []

    def mark(name):
        marks.append((name, P.cnt['dve'], P.cnt['act'], P.cnt['pe']))

    SG = [dscr(f"SG{d}", [AW, NTP]) for d in range(2)]
    AS = [dscr(f"AS{d}", [AW, NTP]) for d in range(2)]
    SCR = {nm: [dscr(f"SC{nm}{d}", [AW, NTP], BF16) for d in range(2)] for nm in ("r", "b", "k", "a")}
    TMB = [dscr(f"TMB{d}", [NT, AW], BF16) for d in range(2)]
    TMKP = [dscr(f"TMKP{d}", [NT, AW], BF16) for d in range(2)]
    TMAP = [dscr(f"TMAP{d}", [NT, AW], BF16) for d in range(2)]
    TMV = dscr("TMV", [NT, AW], BF16)
    PLS = [dscr(f"PLS{d}", [AW, cfg.NCH]) for d in range(2)]
    OD = [dscr(f"OD{d}", [AW, NTP]) for d in range(2)]
    BON = dscr("BON", [AW, NTP])
    NJ = AW // 128

    def cblocks(c0, c1):
        return [(a, min(512, c1 - a)) for a in range(c0, c1, 512)]

    cranges = [(0, cfg.TS + 2, [0]), (cfg.TS + 1, NTP, list(range(1, len(cfg.seqs))))]

    def pcol(si):
        return cfg.seqs[si][0] + si + 1

    def phase_lora(i):
        A.mark()
        wl = alloc([NTP])
        tl = alloc([NTP], BF16)
        wup = alloc([AW], BF16)
        b0 = alloc([2, 2, NJ])
        rows = [alloc([NTP]) for _ in range(2)]
        k.dma("sp", b0[:, 0], e_w0[i])
        k.dma("sp", b0[:, 1], e_a0[i])
        n = 0
        for kind in range(2):
            for d in range(2):
                r0 = 3 * AW + kind * 128 + d * 64
                k.dma("sp", wl[0:64], PT[r0:r0 + 64, :])
                k.act(tl[0:64], wl[0:64], AF.Tanh if kind == 0 else AF.Copy)
                k.dma("pool", wup[0:64], (e_w_up if kind == 0 else e_a_up)[i, d])
                for j in range(NJ):
                    row = rows[n % 2]
                    n += 1
                    for bi, (c0, sz) in enumerate(cblocks(0, NTP)):
                        pb = ps[bi % 8]
                        k.mm(pb[:, 0:sz], [(wup[0:64, j * 128:(j + 1) * 128], tl[0:64, c0:c0 + sz])])
                        k.act(row[:, c0:c0 + sz], pb[:, 0:sz], AF.Sigmoid, bias=b0[:, kind, d, j:j + 1])
                    k.dma("sp", (SG if kind == 0 else AS)[d][j * 128:(j + 1) * 128, :], row)
        A.release()

    xranges = []
    for si_, (off_, Tn_, ci_) in enumerate(cfg.seqs):
        if ci_ == 1:
            n_ = Tn_ // 128
            for n0_ in range(0, n_, 8):
                n1_ = min(n_, n0_ + 8)
                xranges.append((pcol(si_) + n0_ * 128, pcol(si_) + n1_ * 128, [(si_, n0_, n1_)]))
    if len(cfg.seqs) > 1:
        xranges.append((pcol(1), NTP - 1, [(si_, 0, cfg.seqs[si_][1] // 128) for si_ in range(1, len(cfg.seqs))]))
    XW = max(c1_ - c0_ for c0_, c1_, _ in xranges)

    def run_window(jobfns, nslots):
        pending = list(jobfns)
        active = {}
        while pending or active:
            for sl in range(nslots):
                if sl not in active and pending:
                    active[sl] = pending.pop(0)(sl)
            for sl in list(active):
                try:
                    next(active[sl])
                except StopIteration:
                    del active[sl]

    def phase_prep(i):
        A.mark()
        vec = alloc([5, NJ])
        k.dma("sp", vec, e_vec[i])
        NSLOT = 2
        slots = []
        for _ in range(NSLOT):
            sl = dict(rm=alloc([2, XW]), r_=alloc([XW]), kx=alloc([XW]), kk=alloc([XW]), kbar=alloc([XW]),
                      R=[alloc([XW]) for _ in range(7)],
                      ob={nm: alloc([XW], BF16) for nm in ("r", "b", "k", "a", "kp", "ap")},
                      vb=alloc([XW], BF16), sq=alloc([XW], BF16),
                      tms=[alloc([4, 128], BF16) for _ in range(2)], plt=alloc([cfg.NCH]), tmn=[0], rmr=[None])
            slots.append(sl)

        def job(rng, j):
            def gen(sli):
                S = slots[sli]
                c0, c1, segs = rng
                W = c1 - c0
                rm = S["rm"][:, :, 0:W]
                r_, kx, kk, kbar = (S[n_][:, 0:W] for n_ in ("r_", "kx", "kk", "kbar"))
                R = [t_[:, 0:W] for t_ in S["R"]]
                ob = {n_: t_[:, 0:W] for n_, t_ in S["ob"].items()}
                vb = S["vb"][:, 0:W]; sq = S["sq"][:, 0:W]
                tms, plt, tmn = S["tms"], S["plt"], S["tmn"]
                if S["rmr"][0] != (c0, c1):
                    k.dma("sp", rm, cin["rmask"][:, :, c0:c1])
                    S["rmr"][0] = (c0, c1)

                def transposes(src, dst):
                    for (si, n0, n1) in segs:
                        off, Tn, _ = cfg.seqs[si]
                        pc = pcol(si) - c0
                        for cb in range(n0, n1, 4):
                            nb = min(4, n1 - cb)
                            pb = nbank()
                            st = tms[tmn[0] % 2]
                            tmn[0] += 1
                            k.mms([(pb[:, q * 128:(q + 1) * 128], [(src[:, pc + (cb + q) * 128: pc + (cb + q + 1) * 128], identb)])
                                   for q in range(nb)])
                            evac(st[:, 0:nb, :], pb[:, 0:nb * 128].re("p (q c) -> p q c", q=nb))
                            t0 = off + cb * 128
                            k.dma("act", dst[t0:t0 + nb * 128, j * 128:(j + 1) * 128].re("(q p) c -> p q c", p=128), st[:, 0:nb, :])
                            yield

                k.dma("sp", r_, PT[j * 128:(j + 1) * 128, c0:c1])
                k.dma("sp", kx, PT[AW + j * 128:AW + (j + 1) * 128, c0:c1])
                k.dma("sp", R[0], PT[2 * AW + j * 128:2 * AW + (j + 1) * 128, c0:c1])
                k.copy("act", vb, R[0])
                yield from transposes(vb, TMV)
                k.act(kk, kx, AF.Copy, scale=vec[:, 0, j:j + 1])
                k.act(sq, kk, AF.Square)
                yield
                for bi, (a0, sz) in enumerate(cblocks(0, W)):
                    pb = nbank()
                    k.mm(pb[:, 0:sz], [(bones, sq[:, a0:a0 + sz])])
                    k.ts("dve", R[1][:, a0:a0 + sz], pb[:, 0:sz], 1e-24, ALU.max)
                yield
                k.act(R[1], R[1], AF.Ln)
                k.act(R[1], R[1], AF.Exp, scale=-0.5)
                yield
                k.tt("dve", kk, kk, R[1], ALU.mult)
                yield
                for d in range(2):
                    sg, a, cs, cm, em, el, a2 = R
                    k.dma("sp", sg, SG[d][j * 128:(j + 1) * 128, c0:c1])
                    k.dma("sp", a, AS[d][j * 128:(j + 1) * 128, c0:c1])
                    k.tt("dve", a2, kk, a, ALU.mult)
                    k.ts("dve", a, a, -1.0, ALU.add, vec[:, 1, j:j + 1], ALU.mult)
                    yield
                    k.stt("dve", a, a, 1.0, kx, ALU.add, ALU.mult)
                    kd = a
                    if d == 0:
                        k.copy("act", kbar, kd)
                    else:
                        k.tt("dve", kbar, kbar, kd, ALU.add)
                    yield
                    if d == 0:
                        k.scan(cs, rm[:, 0, :], sg, 0.0, ALU.mult, ALU.add)
                    else:
                        k.scan(cs[:, ::-1], rm[:, 1, ::-1], sg[:, ::-1], 0.0, ALU.mult, ALU.add)
                    yield
                    k.tt("dve", cm, cs, sg, ALU.subtract)
                    ep = sg
                    k.act(ep, cs, AF.Exp, scale=-LAM)
                    k.act(em, cs, AF.Exp, scale=LAM)
                    yield
                    k.act(cm, cm, AF.Exp, scale=-LAM)
                    for (si, n0, n1) in segs:
                        off, Tn, _ = cfg.seqs[si]
                        pc = pcol(si) - c0 + n0 * 128
                        nch = n1 - n0
                        Wd = nch * 128
                        e3 = ep[:, pc:pc + Wd].re("p (n l) -> p n l", l=128)
                        edge = e3[:, :, 127:128] if d == 0 else e3[:, :, 0:1]
                        k.tt("dve", el[:, pc:pc + Wd].re("p (n l) -> p n l", l=128),
                             em[:, pc:pc + Wd].re("p (n l) -> p n l", l=128), edge.bcast([128, nch, 128]), ALU.mult)
                        ch = off // 128 + n0
                        k.copy("pool", plt[:, ch:ch + nch], edge.re("p n o -> p (n o)"))
                        k.dma("act", PLS[d][j * 128:(j + 1) * 128, ch:ch + nch], plt[:, ch:ch + nch])
                    yield
                    k.tt("dve", ob["r"], r_, ep, ALU.mult)
                    k.tt("dve", ob["b"], kk, cm, ALU.mult)
                    yield
                    k.tt("dve", ob["k"], kd, em, ALU.mult)
                    k.stt("dve", ob["a"], a2, -1.0, em, ALU.mult, ALU.mult)
                    yield
                    k.tt("dve", ob["kp"], kd, el, ALU.mult)
                    k.stt("dve", ob["ap"], a2, -1.0, el, ALU.mult, ALU.mult)
                    yield
                    for nm in ("r", "b", "k", "a"):
                        k.dma("sp", SCR[nm][d][j * 128:(j + 1) * 128, c0:c1], ob[nm])
                    yield from transposes(ob["b"], TMB[d])
                    yield from transposes(ob["kp"], TMKP[d])
                    yield from transposes(ob["ap"], TMAP[d])
                k.tt("dve", kbar, kbar, r_, ALU.mult)
                k.ts("dve", sq, kbar, vec[:, 2, j:j + 1], ALU.mult, 0.5, ALU.mult)
                k.dma("sp", R[0], PT[2 * AW + j * 128:2 * AW + (j + 1) * 128, c0:c1])
                yield
                for bi, (a0, sz) in enumerate(cblocks(0, W)):
                    pb = nbank()
                    k.mm(pb[:, 0:sz], [(bones, sq[:, a0:a0 + sz])])
                    k.tt("dve", R[1][:, a0:a0 + sz], pb[:, 0:sz], R[0][:, a0:a0 + sz], ALU.mult)
                k.dma("sp", BON[j * 128:(j + 1) * 128, c0:c1], R[1])
                yield
            return gen

        run_window([job(rng, j) for rng in xranges for j in range(NJ)], NSLOT)
        A.release()

    bank_rr = [0]

    def nbank():
        bank_rr[0] = (bank_rr[0] + 1) % 8
        return ps[bank_rr[0]]

    def run_jobs(jobs):
        jobs = list(jobs)
        while jobs:
            nxt = []
            for g in jobs:
                try:
                    next(g)
                    nxt.append(g)
                except StopIteration:
                    pass
            jobs = nxt

    def phase_scan(i):
        A.mark()
        mk = {nm: alloc([4, 128]) for nm in ("m_lt", "m_gt", "m_ge", "m_le", "m_lt_bd", "m_gt_bd", "m_ll", "m_ur")}
        for nm in mk:
            k.dma("sp", mk[nm], cin[nm])

        def blk(pb, nb, w):
            return pb[:, 0:nb * w].re("p (q l) -> p q l", q=nb)

        def blk64(pb, nb, w):
            return pb[0:64, 0:nb * w].re("p (q l) -> p q l", q=nb)

        CH_KB = 36
        chain_base = []
        for _ in range(2):
            chain_base.append(A.p)
            A.p += CH_KB * 1024
        chain_ptr = [0, 0]

        def calloc(slot, shape, dt=F32):
            save = A.p
            A.p = chain_ptr[slot]
            t_ = alloc(shape, dt)
            assert A.p <= chain_base[slot] + CH_KB * 1024, "chain slot overflow"
            chain_ptr[slot] = A.p
            A.p = save
            return t_

        def unit_setup(si, d, hl_, slot):
            off, Tn, ci = cfg.seqs[si]
            n = Tn // 128
            G = len(hl_)
            NV = G * n
            pc = pcol(si)
            ch0 = off // 128
            u = dict(si=si, d=d, hl=hl_, n=NV, n1=n, G=G, pc=pc, Tn=Tn, ci=ci, off=off)
            if d == 0:
                u["masks"] = (mk["m_lt_bd"], mk["m_gt_bd"], mk["m_gt"], mk["m_ge"], mk["m_ll"])
            else:
                u["masks"] = (mk["m_gt_bd"], mk["m_lt_bd"], mk["m_lt"], mk["m_le"], mk["m_ur"])
            for nm in ("r", "b", "k", "a"):
                t_ = alloc([NV * 128], BF16)
                for gi_, h in enumerate(hl_):
                    k.dma("sp", t_[0:64, gi_ * Tn:(gi_ + 1) * Tn], SCR[nm][d][h * 64:(h + 1) * 64, pc:pc + Tn])
                u[nm + "T"] = t_
            BY = alloc([NV, 128], BF16)
            KP = alloc([NV, 64], BF16); APn = alloc([NV, 64], BF16); V = alloc([NV, 64], BF16)
            pl = alloc([NV])
            for gi_, h in enumerate(hl_):
                hs = slice(h * 64, (h + 1) * 64)
                vs_ = slice(gi_ * n, (gi_ + 1) * n)
                k.dma("act", BY[:, vs_, 0:64], TMB[d][off:off + Tn, hs].re("(n t) c -> t n c", t=128))
                k.dma("act", KP[:, vs_, :], TMKP[d][off:off + Tn, hs].re("(n t) c -> t n c", t=128))
                k.dma("act", APn[:, vs_, :], TMAP[d][off:off + Tn, hs].re("(n t) c -> t n c", t=128))
                k.dma("act", V[:, vs_, :], TMV[off:off + Tn, hs].re("(n t) c -> t n c", t=128))
                k.dma("sp", pl[0:64, vs_], PLS[d][hs, ch0:ch0 + n])
            DPt = alloc([NV, 64])
            k.tt("pool", DPt[0:64], identf[0:64, None, 0:64].bcast([64, NV, 64]),
                 pl[0:64, :, None].bcast([64, NV, 64]), ALU.mult)
            SB = calloc(slot, [G, n + 1, 64], BF16)
            S32 = calloc(slot, [G, 64])
            first = 0 if d == 0 else n
            if ci == 1:
                s0 = alloc([G, 64])
                pb = nbank()
                for gi_, h in enumerate(hl_):
                    k.dma("sp", s0[0:64, gi_, :], state_in[i, d, h])
                k.mms([(pb[0:64, gi_ * 64:(gi_ + 1) * 64], [(s0[0:64, gi_, :], identf[0:64, 0:64])]) for gi_ in range(G)])
                k.copy("dve", SB[0:64, :, first, :], pb[0:64, 0:G * 64].re("p (g v) -> p g v", g=G))
            else:
                k.memset("pool", SB[0:64, :, first, :], 0.0)
            OTp = calloc(slot, [G, Tn + 2])
            k.memset("pool", OTp[0:64, :, 0:1], 0.0)
            k.memset("pool", OTp[0:64, :, Tn + 1:Tn + 2], 0.0)
            u.update(BY=BY, KP=KP, APn=APn, V=V, DPt=DPt, SB=SB, S32=S32, OTp=OTp,
                     O0T=calloc(slot, [NV * 128]), RmT=calloc(slot, [NV * 128], BF16), GT=calloc(slot, [NV, 64], BF16),
                     Hh=calloc(slot, [NV, 64]), so=calloc(slot, [G, 64]))
            return u

        def batch_job(u, cb):
            n = u["n"]
            nb = min(4, n - cb)
            cs = list(range(cb, cb + nb))
            mX, mXT, mBK, mRK, mXo = u["masks"]
            rT, bT, kT, aT = u["rT"], u["bT"], u["kT"], u["aT"]
            BY, KP, APn, V = u["BY"], u["KP"], u["APn"], u["V"]
            cl = slice(cb, cb + nb)
            lc = slice(0, nb)
            X = alloc([nb, 128], BF16); XT = alloc([nb, 128], BF16); Xo = alloc([nb, 128], BF16)
            Td = alloc([nb, 128], BF16); P1 = alloc([nb, 128], BF16)
            BKT = alloc([nb, 128], BF16); RKT = alloc([nb, 128], BF16); RAT = alloc([nb, 128], BF16)
            TTb = alloc([nb, 128], BF16)
            Mb = [alloc([nb, 128], BF16) for _ in range(2)]
            MTb = [alloc([nb, 128], BF16) for _ in range(2)]
            WU = alloc([nb, 128], BF16)

            def prod(pb, lt, rt_):
                k.mms([(pb[:, q * 128:(q + 1) * 128],
                        [(lt[0:64, c * 128:(c + 1) * 128], rt_[0:64, c * 128:(c + 1) * 128])])
                       for q, c in enumerate(cs)])

            def sq(pb, lts, rts):
                k.mms([(pb[:, q * 128:(q + 1) * 128], [(lts[:, q, :], rts[:, q, :])]) for q in range(nb)])

            pb = nbank(); prod(pb, bT, aT)
            k.tt("dve", X, blk(pb, nb, 128), mX[:, 0:nb, :], ALU.mult)
            k.tt("dve", Xo, blk(pb, nb, 128), mXo[:, 0:nb, :], ALU.mult)
            yield
            pb = nbank(); prod(pb, aT, bT)
            k.tt("dve", XT, blk(pb, nb, 128), mXT[:, 0:nb, :], ALU.mult)
            k.tt("pool", TTb, XT, identf[:, None, :].bcast([128, nb, 128]), ALU.add)
            yield
            pb = nbank(); prod(pb, kT, bT)
            k.tt("dve", BKT, blk(pb, nb, 128), mBK[:, 0:nb, :], ALU.mult)
            yield
            pb = nbank(); prod(pb, kT, rT)
            k.tt("dve", RKT, blk(pb, nb, 128), mRK[:, 0:nb, :], ALU.mult)
            yield
            pb = nbank(); prod(pb, aT, rT)
            k.tt("dve", RAT, blk(pb, nb, 128), mRK[:, 0:nb, :], ALU.mult)
            yield
            M, MT = X, XT
            for lv in range(5):
                M2, MT2 = Mb[lv % 2], MTb[lv % 2]
                pb = nbank(); sq(pb, MT, M)
                k.copy("act", M2, blk(pb, nb, 128))
                if lv < 4:
                    pb = nbank(); sq(pb, M, MT)
                    k.copy("dve", MT2, blk(pb, nb, 128))
                yield
                pb = nbank(); sq(pb, M2, TTb)
                k.tt("dve", TTb, TTb, blk(pb, nb, 128), ALU.add)
                M, MT = M2, MT2
                yield
            pb = nbank()
            k.mms([(pb[:, q * 128:(q + 1) * 128], [(TTb[:, q, :], identb)]) for q in range(nb)])
            k.copy("act", Td, blk(pb, nb, 128))
            pb = nbank(); sq(pb, Xo, TTb)
            k.copy("dve", P1, blk(pb, nb, 128))
            yield
            pb = nbank(); sq(pb, Td, P1)
            k.tt("dve", TTb, TTb, blk(pb, nb, 128), ALU.add)
            yield
            pb = nbank()
            k.mms([(pb[:, q * 64:(q + 1) * 64], [(BKT[:, q, :], V[:, c, :])]) for q, c in enumerate(cs)])
            k.copy("act", BY[:, cl, 64:128], blk(pb, nb, 64))
            yield
            pb = nbank()
            k.mms([(pb[:, q * 128:(q + 1) * 128], [(TTb[:, q, :], BY[:, c, :])]) for q, c in enumerate(cs)])
            k.copy("act", WU, blk(pb, nb, 128))
            yield
            pb = nbank()
            k.mms([(pb[0:64, q * 128:(q + 1) * 128],
                    [(V[:, c, :], RKT[:, q, :]), (WU[:, q, 64:128], RAT[:, q, :])]) for q, c in enumerate(cs)])
            k.copy("dve", u["O0T"][0:64, cb * 128:(cb + nb) * 128], pb[0:64, 0:nb * 128])
            pb = nbank()
            k.mms([(pb[0:64, q * 128:(q + 1) * 128],
                    [(WU[:, q, 0:64], RAT[:, q, :]), (identb[0:64, 0:64], rT[0:64, c * 128:(c + 1) * 128])])
                   for q, c in enumerate(cs)])
            k.copy("act", u["RmT"][0:64, cb * 128:(cb + nb) * 128], pb[0:64, 0:nb * 128])
            yield
            pb = nbank()
            k.mms([(pb[0:64, q * 64:(q + 1) * 64], [(WU[:, q, 0:64], APn[:, c, :])]) for q, c in enumerate(cs)])
            k.tt("dve", u["GT"][0:64, cl, :], blk64(pb, nb, 64), u["DPt"][0:64, cl, :], ALU.add)
            pb = nbank()
            k.mms([(pb[0:64, q * 64:(q + 1) * 64],
                    [(KP[:, c, :], V[:, c, :]), (APn[:, c, :], WU[:, q, 64:128])]) for q, c in enumerate(cs)])
            k.copy("act", u["Hh"][0:64, cl, :], blk64(pb, nb, 64))
            yield

        def chain_job(u):
            n, d, ci, G = u["n1"], u["d"], u["ci"], u["G"]
            SB, S32 = u["SB"], u["S32"]
            GT4 = u["GT"].re("p (g c) v -> p g c v", g=G)
            H4 = u["Hh"].re("p (g c) v -> p g c v", g=G)
            order = list(range(n)) if d == 0 else list(range(n - 1, -1, -1))
            for ii, c in enumerate(order):
                before = c if d == 0 else c + 1
                after = c + 1 if d == 0 else c
                pb = nbank()
                k.mms([(pb[0:64, g * 64:(g + 1) * 64], [(GT4[0:64, g, c, :], SB[0:64, g, before, :])]) for g in range(G)])
                pv = pb[0:64, 0:G * 64].re("p (g v) -> p g v", g=G)
                k.tt("dve", SB[0:64, :, after, :], pv, H4[0:64, :, c, :], ALU.add)
                if ii == n - 1 and ci == 0:
                    k.tt("dve", S32[0:64], pv, H4[0:64, :, c, :], ALU.add)
                yield
            if ci == 0:
                pb = nbank()
                k.mms([(pb[0:64, g * 64:(g + 1) * 64], [(S32[0:64, g, :], identf[0:64, 0:64])]) for g in range(G)])
                so = u["so"]
                k.copy("act", so[0:64], pb[0:64, 0:G * 64].re("p (g v) -> p g v", g=G))
                h0_ = u["hl"][0]
                k.dma("sp", st_out[u["si"] - 1, i, d, h0_:h0_ + G].re("h v k -> v h k"), so[0:64])
                yield

        def out_job(u):
            n, d, G, NV, Tn = u["n1"], u["d"], u["G"], u["n"], u["Tn"]
            SB, RmT, O0T, OTp = u["SB"], u["RmT"], u["O0T"], u["OTp"]
            for cb in range(0, NV, 4):
                nb = min(4, NV - cb)
                cs = list(range(cb, cb + nb))
                pb = nbank()
                k.mms([(pb[0:64, q * 128:(q + 1) * 128],
                        [(SB[0:64, cv // n, ((cv % n) if d == 0 else (cv % n) + 1), :], RmT[0:64, cv * 128:(cv + 1) * 128])])
                       for q, cv in enumerate(cs)])
                g0_ = cb // n
                if n >= 4:
                    c0_ = (cb % n) * 128
                    k.tt("dve", OTp[0:64, g0_, 1 + c0_:1 + c0_ + nb * 128], pb[0:64, 0:nb * 128],
                         O0T[0:64, cb * 128:(cb + nb) * 128], ALU.add)
                else:
                    ng = nb // n
                    k.tt("dve", OTp[0:64, g0_:g0_ + ng, 1:1 + Tn], pb[0:64, 0:nb * 128].re("p (g t) -> p g t", g=ng),
                         O0T[0:64, cb * 128:(cb + nb) * 128].re("p (g t) -> p g t", g=ng), ALU.add)
                yield
            for gi_, h in enumerate(u["hl"]):
                k.dma("sp", OD[d][h * 64:(h + 1) * 64, u["pc"] - 1:u["pc"] + Tn + 1], OTp[0:64, gi_, :])

        def tail_job(us):
            gens = [chain_job(u) for u in us]
            while gens:
                nxt = []
                for g_ in gens:
                    try:
                        next(g_)
                        nxt.append(g_)
                    except StopIteration:
                        pass
                gens = nxt
                yield
            gens = [out_job(u) for u in us]
            while gens:
                nxt = []
                for g_ in gens:
                    try:
                        next(g_)
                        nxt.append(g_)
                    except StopIteration:
                        pass
                gens = nxt
                yield

        groups = []
        for si, (off, Tn, ci) in enumerate(cfg.seqs):
            n = Tn // 128
            G = 1 if n > 4 else 4
            for d in range(2):
                for h0 in range(0, cfg.AH, G):
                    groups.append((si, d, list(range(h0, min(cfg.AH, h0 + G)))))
        steps = []
        gi = 0
        while gi < len(groups):
            si, d, hs_ = groups[gi]
            if cfg.seqs[si][1] // 128 <= 2 and gi + 1 < len(groups) and cfg.seqs[groups[gi + 1][0]][1] // 128 <= 2:
                steps.append([groups[gi], groups[gi + 1]])
                gi += 2
            else:
                steps.append([groups[gi]])
                gi += 1
        prev = None
        for sti, grp in enumerate(steps):
            slot = sti % 2
            chain_ptr[slot] = chain_base[slot]
            A.mark()
            us = [unit_setup(si, d, hs_, slot) for (si, d, hs_) in grp]
            jobs = [batch_job(u, cb) for u in us for cb in range(0, u["n"], 4)]
            if prev is not None:
                jobs.append(tail_job(prev))
            run_jobs(jobs)
            A.release()
            prev = us
        run_jobs([tail_job(prev)])
        A.release()

    def phase_gn_job(i):
        vec = alloc([5, NJ])
        k.dma("sp", vec, e_vec[i])
        W0 = XW
        o_ = alloc([W0]); o2_ = alloc([W0]); ob16_ = alloc([W0], BF16); cen_ = alloc([W0]); rstd_ = alloc([W0])
        bon_ = alloc([W0]); ga_ = alloc([W0]); yo_ = alloc([W0], BF16)

        def gen():
            for (c0, c1, segs_) in xranges:
                W = c1 - c0
                o, o2, ob16, cen, rstd, bon, ga, yo = (t_[:, 0:W] for t_ in (o_, o2_, ob16_, cen_, rstd_, bon_, ga_, yo_))
                for j in range(NJ):
                    rs_ = slice(j * 128, (j + 1) * 128)
                    k.dma("sp", o, OD[0][rs_, c0:c1])
                    k.dma("sp", o2, OD[1][rs_, c0:c1])
                    k.dma("act", bon, BON[rs_, c0:c1])
                    k.dma("act", ga, PT[cfg.SHIFT + j * 128: cfg.SHIFT + (j + 1) * 128, c0:c1])
                    k.tt("dve", o, o, o2, ALU.add)
                    k.copy("act", ob16, o)
                    yield
                    for bi, (a0, sz) in enumerate(cblocks(0, W)):
                        pb = nbank()
                        k.mm(pb[:, 0:sz], [(bones, ob16[:, a0:a0 + sz])])
                        k.stt("dve", cen[:, a0:a0 + sz], pb[:, 0:sz], -1.0 / 64, o[:, a0:a0 + sz], ALU.mult, ALU.add)
                    yield
                    k.act(ob16, cen, AF.Square)
                    for bi, (a0, sz) in enumerate(cblocks(0, W)):
                        pb = nbank()
                        k.mm(pb[:, 0:sz], [(bones, ob16[:, a0:a0 + sz])])
                        k.ts("dve", rstd[:, a0:a0 + sz], pb[:, 0:sz], 1.0 / 64, ALU.mult, 64e-5, ALU.add)
                    yield
                    k.act(rstd, rstd, AF.Ln)
                    k.act(rstd, rstd, AF.Exp, scale=-0.5)
                    yield
                    k.tt("dve", cen, cen, rstd, ALU.mult)
                    k.act(cen, cen, AF.Identity, scale=vec[:, 3, j:j + 1], bias=vec[:, 4, j:j + 1])
                    yield
                    k.tt("dve", cen, cen, bon, ALU.add)
                    k.act(ga, ga, AF.Silu)
                    yield
                    k.tt("dve", yo, cen, ga, ALU.mult)
                    k.dma("sp", YT[rs_, c0:c1], yo)
                    yield
        return gen()

    def phase_fft_job(i):
        A.mark()
        BGC, BG = cfg.BGC, cfg.BG
        KC = BGC // 128
        dc = alloc([KC, 2 * BGC], BF16)
        k.dma("sp", dc, cin["dftc"].re("(k p) c -> p k c", p=128))
        u0 = cfg.SHIFT + AW
        g0 = u0 + BW
        for si, (off, Tn, ci) in enumerate(cfg.seqs):
            A.mark()
            pc = pcol(si)
            n = Tn // 128
            ab = alloc([n, BG, 2 * BGC], BF16)
            ub = [alloc([Tn], BF16) for _ in range(2)]
            uf = [alloc([Tn]) for _ in range(2)]
            q = 0
            for g in range(BG):
                us = []
                for kc in range(KC):
                    r0 = u0 + g * BGC + kc * 128
                    f_, b_ = uf[kc % 2], ub[kc % 2]
                    k.dma("sp", f_, PT[r0:r0 + 128, pc:pc + Tn])
                    k.copy("pool", b_, f_)
                    us.append(b_)
                assert KC <= 2
                for t in range(n):
                    for hb in range(0, 2 * BGC, 512):
                        pb = nbank()
                        q += 1
                        k.mm(pb[:, :], [(us[kc][:, t * 128:(t + 1) * 128], dc[:, kc, hb:hb + 512]) for kc in range(KC)])
                        evac(ab[:, t, g, hb:hb + 512], pb[:, :])
                    yield
            TB = min(512, Tn)
            ct = [alloc([n, TB], BF16)] * 2
            stt_ = [alloc([n, TB], BF16)] * 2
            gt = [alloc([TB]) for _ in range(2)]
            yo = [alloc([TB], BF16) for _ in range(2)]
            q = 0
            for tb in range(0, Tn, TB):
                c_, s_ = ct[(tb // TB) % 2], stt_[(tb // TB) % 2]
                k.dma("act", c_, cin[f"dct{Tn}"][:, tb:tb + TB].re("(n p) t -> p n t", p=128))
                k.dma("act", s_, cin[f"dst{Tn}"][:, tb:tb + TB].re("(n p) t -> p n t", p=128))
                for g in range(BG):
                    for cc in range(BGC // 128):
                        pb = nbank()
                        g_ = gt[q % 2]
                        y_ = yo[q % 2]
                        q += 1
                        row = g * BGC + cc * 128
                        k.dma("sp", g_, PT[g0 + row: g0 + row + 128, pc + tb: pc + tb + TB])
                        k.act(g_, g_, AF.Silu)
                        pairs = []
                        for t in range(n):
                            pairs.append((ab[:, t, g, cc * 128:(cc + 1) * 128], c_[:, t, :]))
                            pairs.append((ab[:, t, g, BGC + cc * 128: BGC + (cc + 1) * 128], s_[:, t, :]))
                        k.mm(pb[:, 0:TB], pairs)
                        k.tt("dve", y_, pb[:, 0:TB], g_, ALU.mult)
                        k.dma("sp", YT[AW + row: AW + row + 128, pc + tb: pc + tb + TB], y_)
                        yield
            A.release()
        A.release()


    KVW = cfg.CKV * 64
    QKR = dscr("QKR", [D + KVW, NTP], BF16)
    VTM = dscr("VTM", [NT, KVW], BF16)

    def phase_attn(i):
        HT = HTbox[0]
        A.mark()
        TS = cfg.TS
        pc0 = pcol(0)
        cosT = alloc([TS]); sinT = alloc([TS])
        k.dma("sp", cosT, cin["ropec"])
        k.dma("sp", sinT, cin["ropes"])
        xr = [alloc([NTP]) for _ in range(2)]
        xs = [alloc([TS]) for _ in range(2)]
        t1 = alloc([TS]); t2 = alloc([TS])
        orow = [alloc([NTP], BF16) for _ in range(2)]
        for c in range((D + KVW) // 128):
            X = xr[c % 2]; Xs = xs[c % 2]; o = orow[c % 2]
            k.dma("sp", X, PT[c * 128:(c + 1) * 128, :])
            for b in range(8):
                k.dma("act", Xs[b * 16:(b + 1) * 16], PT[c * 128 + (b ^ 1) * 16: c * 128 + (b ^ 1) * 16 + 16, pc0:pc0 + TS])
            k.copy("act", o, X)
            k.tt("dve", t1, X[:, pc0:pc0 + TS], cosT, ALU.mult)
            k.tt("pool", t2, Xs, sinT, ALU.mult)
            k.tt("dve", o[:, pc0:pc0 + TS], t1, t2, ALU.add)
            k.dma("sp", QKR[c * 128:(c + 1) * 128, :], o)
        A.release()
        if stop_after == ("attn1", 1):
            raise StopBuild()
        mark('attn_O2')
        A.mark()
        wkv = alloc([KD, 2 * KVW], BF16)
        k.dma("pool", wkv, o_w_in[i][:, D:D + 2 * KVW].re("(k p) c -> p k c", p=128))
        vf = [alloc([KVW]) for _ in range(2)]
        vb = [alloc([KVW], BF16) for _ in range(2)]
        kf = [alloc([KVW]) for _ in range(2)]
        for n_, (ti, ci, si) in enumerate(tiles):
            if n_ >= DBGN:
                break
            off, Tn, _ = cfg.seqs[si]
            tl = slice(ti * 128, (ti + 1) * 128)
            pb = ps[n_ % 4]
            k.mm(pb[:, 0:KVW], [(HT[:, kk, tl], wkv[:, kk, KVW:2 * KVW]) for kk in range(KD)])
            v_ = vf[n_ % 2]; b_ = vb[n_ % 2]
            if DBG == -1:
                continue
            k.copy("act", v_, pb[:, 0:KVW])
            if DBG == -2:
                continue
            k.copy("pool", b_, v_)
            if DBG < 1:
                continue
            k.dma("sp", VTM[tl, :], b_)
            if DBG < 2:
                continue
            if ci == 0:
                t0 = ti * 128 - off
                k.dma("sp", nv_out[si - 1, i, t0:t0 + 128, :], v_)
                if DBG < 3:
                    continue
                pb2 = ps[4 + n_ % 4]
                k.mm(pb2[:, 0:KVW], [(HT[:, kk, tl], wkv[:, kk, 0:KVW]) for kk in range(KD)])
                k_ = kf[n_ % 2]
                k.copy("act", k_, pb2[:, 0:KVW])
                k.dma("sp", nk_out[si - 1, i, t0:t0 + 128, :], k_)
        A.release()
        A.release()
        if stop_after == ("attn2", 1):
            raise StopBuild()
        mark('attn_O3')
        A.mark()
        NPT = cfg.PAST // 128
        ckb = alloc([NPT, KVW], BF16); cvb = alloc([NPT, cfg.CKV, 128], BF16)
        k.dma("pool", ckb, ck_in[i].re("(n p) c -> p n c", p=128))
        k.memset("pool", cvb[:, :, :, 64:128], 1.0)
        for pt_ in range(NPT):
            k.dma("pool", cvb[:, pt_, :, 0:64], cv_in[i][pt_ * 128:(pt_ + 1) * 128, :].re("p (h c) -> p h c", c=64))
        skr = alloc([128], BF16)
        k.memset("pool", skr[0:1, 0:64], 0.0)
        k.memset("pool", skr[0:1, 64:128], 1.0)
        CKT = alloc([cfg.CKV, cfg.PAST], BF16)
        q_ = 0
        for pt in range(NPT):
            for kvh in range(cfg.CKV):
                pb = ps[q_ % 8]; q_ += 1
                k.mm(pb[0:64, 0:128], [(ckb[:, pt, kvh * 64:(kvh + 1) * 64], identb)])
                evac(CKT[0:64, kvh, pt * 128:(pt + 1) * 128], pb[0:64, 0:128])
        snk = alloc([cfg.CH]); esr = alloc([cfg.CH, 128], BF16)
        k.dma("sp", snk[0:1], o_sink[i])
        k.act(snk[0:1], snk[0:1], AF.Exp)
        k.copy("dve", esr[0:1], snk[0:1, :, None].bcast([1, cfg.CH, 128]))
        mp = alloc([4, 128], BF16); mn = alloc([4, 128], BF16)
        k.dma("sp", mp, cin["amaskp"])
        k.dma("sp", mn, cin["amaskn"])
        goff = D + 2 * KVW
        NSLOT = 2
        slots = []
        TM = max(Tn_ for (_, Tn_, _) in cfg.seqs)
        for sl_ in range(NSLOT):
            vt_ = alloc([TM // 128, 128], BF16)
            k.memset("pool", vt_[:, :, 64:128], 1.0)
            slots.append(dict(kT=alloc([TM], BF16), qg=alloc([4, TM], BF16), vt=vt_,
                              gg=alloc([4, TM], BF16), yst=alloc([4, TM], BF16),
                              Pb=[alloc([512], BF16) for _ in range(6)], rc=[alloc([512]) for _ in range(2)],
                              tq=[alloc([512]) for _ in range(2)], pn=[0]))

        def ujob(si, kvh):
            def gen(sli):
                S = slots[sli]
                off, Tn, ci = cfg.seqs[si]
                n = Tn // 128
                pc = pcol(si)
                kT = S["kT"][:, 0:Tn]; qg = S["qg"][:, :, 0:Tn]; vt = S["vt"][:, 0:n, :]
                gg = S["gg"][:, :, 0:Tn]; yst = S["yst"][:, :, 0:Tn]
                Pb, rc, tq, pn = S["Pb"], S["rc"], S["tq"], S["pn"]
                bs = [ps[4 * sli], ps[4 * sli + 1]]
                pOT, pRS = ps[4 * sli + 2], ps[4 * sli + 3]
                k.dma("sp", kT[0:64], QKR[D + kvh * 64: D + (kvh + 1) * 64, pc:pc + Tn])
                for g in range(4):
                    hq = kvh * 4 + g
                    k.dma("sp", qg[0:64, g, :], QKR[hq * 64:(hq + 1) * 64, pc:pc + Tn])
                    k.dma("pool", gg[0:64, g, :], PT[goff + hq * 64: goff + (hq + 1) * 64, pc:pc + Tn])
                k.dma("act", vt[:, :, 0:64], VTM[off:off + Tn, kvh * 64:(kvh + 1) * 64].re("(n t) c -> t n c", t=128))
                k.act(gg[0:64], gg[0:64], AF.Silu)
                yield
                for qb in range(n):
                    rhs = qg[0:64, :, qb * 128:(qb + 1) * 128]
                    if ci == 1:
                        keys = [("s", j) for j in (qb - 1, qb, qb + 1) if 0 <= j < n] + [("c", pt) for pt in range(NPT)]
                    else:
                        keys = [("s", j) for j in range(n)]
                    Ps = []
                    vs = []
                    for jj, (kind, j) in enumerate(keys):
                        pbk = bs[jj % 2]
                        msk = None
                        if kind == "s":
                            pairs = [(kT[0:64, j * 128:(j + 1) * 128], rhs)]
                            if ci == 1 and j == qb - 1:
                                msk = mp
                            if ci == 1 and j == qb + 1:
                                msk = mn
                            vs.append(vt[:, j, 0:64])
                        else:
                            pairs = [(CKT[0:64, kvh, j * 128:(j + 1) * 128], rhs)]
                            vs.append(cvb[:, j, kvh, 0:64])
                        k.mm(pbk[:, :], pairs)
                        P_ = Pb[pn[0] % 6]; pn[0] += 1
                        k.act(P_, pbk[:, :], AF.Exp, scale=0.125)
                        if msk is not None:
                            k.tt("dve", P_, P_, msk.re("p g q -> p (g q)"), ALU.mult)
                        Ps.append(P_)
                        yield
                    k.mm(pOT[0:64, :], [(v_, P_) for v_, P_ in zip(vs, Ps)])
                    k.mm(pRS[0:64, :], [(onesb, P_) for P_ in Ps] + [(onesb[0:1, 0:64], esr[0:1, kvh * 4:(kvh + 1) * 4, :])])
                    r_ = rc[qb % 2]; t_ = tq[qb % 2]
                    k.act(r_[0:64], pRS[0:64, :], AF.Ln)
                    k.act(r_[0:64], r_[0:64], AF.Exp, scale=-1.0)
                    k.tt("dve", t_[0:64], pOT[0:64, :], r_[0:64], ALU.mult)
                    k.tt("pool", yst[0:64, :, qb * 128:(qb + 1) * 128], t_[0:64].re("p (g q) -> p g q", g=4),
                         gg[0:64, :, qb * 128:(qb + 1) * 128], ALU.mult)
                    yield
                for g in range(4):
                    hq = kvh * 4 + g
                    k.dma("sp", YT[hq * 64:(hq + 1) * 64, pc:pc + Tn], yst[0:64, g, :])
                yield
            return gen

        run_window([ujob(si, kvh) for si in range(len(cfg.seqs)) for kvh in range(cfg.CKV)], NSLOT)
        A.release()


    def main():
        for l in range(cfg.DEPTH):
            i = l // 2
            src = x_in if l == 0 else XR
            dst = y_out if l == cfg.DEPTH - 1 else XR
            XRsrc[0] = src
            mark('mod')
            phase_mod(l)
            A.mark()
            HTbox[0] = alloc([KD, NT], BF16)
            mark('norm')
            phase_norm(l, src)
            if l % 2 == 0:
                mark('inproj')
                phase_inproj(e_w_in[i], cfg.EIN, cfg.SHIFT, e_mu[i])
                A.release()
                if stop_after == ("inproj", l):
                    return
                mark('lora')
                phase_lora(i)
                mark('prep')
                phase_prep(i)
                if stop_after == ("prep", l):
                    return
                mark('scan')
                phase_scan(i)
                if stop_after == ("scan", l):
                    return
                mark('gn')
                A.mark()
                gj = phase_gn_job(i)
                run_jobs([gj, phase_fft_job(i)])
                A.release()
                mark('fft')
                if stop_after == ("mix", l):
                    return
                mark('outproj')
                phase_outproj(l, e_w_out[i], YT, dst)
            else:
                mark('inproj')
                phase_inproj(o_w_in[i], cfg.OIN, 0, None)
                mark('attn')
                phase_attn(i)
                mark('outproj')
                phase_outproj(l, o_w_out[i], YT, dst)
            if stop_after == ("layer", l):
                return
        return

    try:
        main()
    except StopBuild:
        pass
    P.barrier()
    P.emit()
    global LAST_STATS
    mark('end')
    LAST_STATS = dict(ops=dict(P.cnt), dmas=dict(P.dn), nsem=P.nsem, marks=marks)
    return nc


def pmajor(v, n=128):
    sh = v.shape
    return np.ascontiguousarray(np.swapaxes(v.reshape(sh[:-1] + (sh[-1] // n, n)), -1, -2))


def make_in_maps(cfg, inp, ncores):
    f = np.float32
    shared = {}
    shared["mod_w"] = np.ascontiguousarray(inp["mod_w"], f)
    shared["mod_b"] = np.ascontiguousarray(inp["mod_b"], f)
    shared["norm_pre"] = np.ascontiguousarray(inp["norm_pre"], f)
    shared["norm_post"] = np.ascontiguousarray(inp["norm_post"], f)
    shared["even_w_in"] = np.ascontiguousarray(inp["even_w_in"], f)
    shared["even_mu"] = pmajor(np.asarray(inp["even_mu"], f))
    shared["even_w0"] = np.ascontiguousarray(np.moveaxis(pmajor(np.asarray(inp["even_w0"], f)), 1, 2))
    shared["even_a0"] = np.ascontiguousarray(np.moveaxis(pmajor(np.asarray(inp["even_a0"], f)), 1, 2))
    shared["even_w_up"] = np.ascontiguousarray(inp["even_w_up"], f)
    shared["even_a_up"] = np.ascontiguousarray(inp["even_a_up"], f)
    vec = np.stack([np.asarray(inp["even_k_k"], f), np.asarray(inp["even_k_a"], f),
                    np.asarray(inp["even_r_k"], f).reshape(cfg.NE, cfg.AW),
                    np.asarray(inp["even_gn_w"], f), np.asarray(inp["even_gn_b"], f)], 1)
    shared["even_vec"] = np.ascontiguousarray(np.moveaxis(pmajor(vec), 1, 2))
    shared["even_w_out"] = np.ascontiguousarray(inp["even_w_out"], f)
    shared["odd_w_in"] = np.ascontiguousarray(inp["odd_w_in"], f)
    shared["odd_sink"] = np.ascontiguousarray(np.asarray(inp["odd_sink"], f)[:, None, :])
    shared["odd_w_out"] = np.ascontiguousarray(inp["odd_w_out"], f)
    for n, v in host_consts(cfg).items():
        shared["c_" + n] = v
    maps = []
    cctx = np.asarray(inp["c_ctx"], f)
    for c in range(ncores):
        m = dict(shared)
        xs = np.asarray(inp["x_sample"][c], f)
        xp = np.asarray(inp["x_prompt"][c * cfg.NP:(c + 1) * cfg.NP], f).reshape(cfg.NP * cfg.TP, cfg.D)
        m["x"] = np.ascontiguousarray(np.concatenate([xs, xp], 0))
        cc = np.stack([cctx, np.asarray(inp["c"][c], f)], -1)
        m["cond"] = np.ascontiguousarray(cc.reshape(cfg.KD, 128, 2).transpose(1, 0, 2))
        m["state"] = np.ascontiguousarray(inp["state_wkv"][c], f)
        m["cache_k"] = np.ascontiguousarray(np.asarray(inp["cache_k"][c], f).reshape(cfg.NO, cfg.PAST, cfg.CKV * 64))
        m["cache_v"] = np.ascontiguousarray(np.asarray(inp["cache_v"][c], f).reshape(cfg.NO, cfg.PAST, cfg.CKV * 64))
        maps.append(m)
    return maps


NCORES = 8
import os as _os
DBG = int(_os.environ.get('KDBG', '9'))
DBGN = int(_os.environ.get('KDBGN', '999'))
LAST_STATS = None


def kernel(**inputs):
    cfg = Cfg()
    inp = {k_: np.asarray(v) for k_, v in inputs.items()}
    nc = build(cfg)
    maps = make_in_maps(cfg, inp, NCORES)
    res = run_bass_kernel_spmd(nc, maps, core_ids=list(range(NCORES)))
    rs = res.results
    B, S, D = NCORES * cfg.NP, cfg.TP, cfg.D
    y_sample = np.stack([np.asarray(r["y"][:cfg.TS], np.float32) for r in rs], 0)
    y_prompt = np.concatenate([np.asarray(r["y"][cfg.TS:], np.float32).reshape(cfg.NP, cfg.TP, D) for r in rs], 0)
    new_state = np.concatenate([np.asarray(r["new_state"], np.float32) for r in rs], 0)
    new_k = np.concatenate([np.asarray(r["new_k"], np.float32) for r in rs], 0).reshape(B, cfg.NO, S, cfg.CKV, 64)
    new_v = np.concatenate([np.asarray(r["new_v"], np.float32) for r in rs], 0).reshape(B, cfg.NO, S, cfg.CKV, 64)
    return (y_prompt, y_sample, new_state, new_k, new_v)
```
